# Optimizing a Trainium2 kernel written in Bass

```python
import jax
import jax.numpy as jnp
from jax import lax
import numpy as np

D_MODEL = 1024
BATCH = 4
SEQ = 4096
DEPTH = 2

GRID_W = 64
CTX_LEN = 256
HEAD_DIM = 64
GLA_HEADS = 4
NA_HEADS = 6
SWA_HEADS = 6
SWA_KV_HEADS = 2
GLA_W = GLA_HEADS * HEAD_DIM
NA_W = NA_HEADS * HEAD_DIM
SWA_W = SWA_HEADS * HEAD_DIM
SWA_KV_W = SWA_KV_HEADS * HEAD_DIM
MIX_WIDTH = GLA_W + NA_W + SWA_W
GLA_RANK = 16
GLA_TAU = 16.0
GLA_CHUNK = 64
NA_KH = 8
NA_KW = 16
NA_QW = 16
NA_CB = NA_KW + NA_QW
SWA_WINDOW = 128
SWA_BLOCK = 128
ROPE_THETA = 10000.0
NORM_EPS = 1e-6
D_FF = -(-8 * D_MODEL // (3 * 256)) * 256
IN_SPLITS = (GLA_W, GLA_W, GLA_W, GLA_W, GLA_RANK, GLA_RANK, NA_W, NA_W, NA_W, SWA_W, SWA_KV_W, SWA_KV_W)
IN_WIDTH = sum(IN_SPLITS)

kernel_name = 'hybrid_parallel_head_groups_dit'


def rms_norm(x, g):
    xf = x.astype(jnp.float32)
    y = xf * lax.rsqrt(jnp.mean(xf * xf, axis=-1, keepdims=True) + NORM_EPS)
    return (y * g.astype(jnp.float32)).astype(x.dtype)


def split_cols(p):
    bounds = [int(b) for b in np.cumsum(IN_SPLITS)[:-1]]
    return jnp.split(p, bounds, axis=-1)


def split_heads(a, n_heads):
    return a.reshape(a.shape[:-1] + (n_heads, HEAD_DIM))


def axial_rope(n_tokens):
    t = jnp.arange(n_tokens, dtype=jnp.int32)
    row = (t // GRID_W).astype(jnp.float32)
    col = (t % GRID_W).astype(jnp.float32)
    n_freq = HEAD_DIM // 4
    inv_freq = ROPE_THETA ** (-jnp.arange(n_freq, dtype=jnp.float32) / n_freq)
    ang = jnp.concatenate([row[:, None] * inv_freq, col[:, None] * inv_freq], axis=-1)
    return jnp.cos(ang), jnp.sin(ang)


def apply_rope(a, cos, sin):
    half = HEAD_DIM // 2
    af = a.astype(jnp.float32)
    a1, a2 = af[..., :half], af[..., half:]
    cs = cos[None, :, None, :]
    sn = sin[None, :, None, :]
    return jnp.concatenate([a1 * cs - a2 * sn, a1 * sn + a2 * cs], axis=-1).astype(a.dtype)


def gla_heads(a):
    B, T, _ = a.shape
    return a.reshape(B, T, GLA_HEADS, HEAD_DIM).transpose(0, 2, 1, 3)


def gla_log_decay(z_low, w2, b2):
    z = (z_low @ w2 + b2).astype(jnp.float32)
    return gla_heads(jax.nn.log_sigmoid(z) / GLA_TAU)


def gla_chunked(q, k, v, log_a, s0):
    B, H, T, dk = q.shape
    dv = v.shape[-1]
    n = T // GLA_CHUNK

    def chunks(a):
        return a.astype(jnp.float32).reshape(B, H, n, GLA_CHUNK, a.shape[-1])

    qc, kc, vc, la = chunks(q), chunks(k), chunks(v), chunks(log_a)
    b = jnp.cumsum(la, axis=3)
    b_last = b[:, :, :, -1:, :]
    q_in = qc * jnp.exp(b)
    k_in = kc * jnp.exp(-b)
    k_end = kc * jnp.exp(b_last - b)
    lower = jnp.tril(jnp.ones((GLA_CHUNK, GLA_CHUNK), dtype=bool))
    a_intra = jnp.where(lower, jnp.einsum('bhncd,bhnsd->bhncs', q_in, k_in), 0.0)
    o_intra = jnp.einsum('bhncs,bhnse->bhnce', a_intra, vc)
    chunk_state = jnp.einsum('bhnsd,bhnse->bhnde', k_end, vc)
    chunk_decay = jnp.exp(b_last[:, :, :, 0, :])

    def step(state, inp):
        dec, cs = inp
        return dec[..., None] * state + cs, state

    s_final, s_enter = lax.scan(step, s0, (jnp.moveaxis(chunk_decay, 2, 0), jnp.moveaxis(chunk_state, 2, 0)))
    o_inter = jnp.einsum('bhncd,nbhde->bhnce', q_in, s_enter)
    o = (o_intra + o_inter).reshape(B, H, T, dv).astype(v.dtype)
    return o, s_final


def gla_final_state(k, v, log_a):
    A = jnp.cumsum(log_a.astype(jnp.float32), axis=2)
    w = jnp.exp(A[:, :, -1:, :] - A)
    return jnp.einsum('bhtd,bhte->bhde', k.astype(jnp.float32) * w, v.astype(jnp.float32))


def gla_output(o, g, norm_g):
    B, H, T, dv = o.shape
    o = rms_norm(o.transpose(0, 2, 1, 3), norm_g.reshape(H, dv))
    return o.reshape(B, T, H * dv) * jax.nn.silu(g)


def gla_mixer(parts, parts_c, wa2_f, ba_f, wa2_b, ba_b, norm_g, need_ctx_out):
    q, k, v, g, za_f, za_b = parts
    qc, kc, vc, gc, zac_f, zac_b = parts_c
    scale = HEAD_DIM ** -0.5

    def flip(a):
        return jnp.flip(a, axis=2)

    K_c, V_c = gla_heads(kc), gla_heads(vc)
    la_cf = gla_log_decay(zac_f, wa2_f, ba_f)
    la_cb = gla_log_decay(zac_b, wa2_b, ba_b)
    if need_ctx_out:
        Q_c = gla_heads(qc) * scale
        zeros = jnp.zeros(K_c.shape[:2] + (HEAD_DIM, HEAD_DIM), jnp.float32)
        oc_f, s_f = gla_chunked(Q_c, K_c, V_c, la_cf, zeros)
        oc_b, s_b = gla_chunked(flip(Q_c), flip(K_c), flip(V_c), flip(la_cb), zeros)
        yc = gla_output(oc_f + flip(oc_b), gc, norm_g)
    else:
        s_f = gla_final_state(K_c, V_c, la_cf)
        s_b = gla_final_state(flip(K_c), flip(V_c), flip(la_cb))
        yc = None
    Q, K, V = gla_heads(q) * scale, gla_heads(k), gla_heads(v)
    o_f, _ = gla_chunked(Q, K, V, gla_log_decay(za_f, wa2_f, ba_f), s_f)
    o_b, _ = gla_chunked(flip(Q), flip(K), flip(V), flip(gla_log_decay(za_b, wa2_b, ba_b)), s_b)
    y = gla_output(o_f + flip(o_b), g, norm_g)
    return y, yc


def context_attention(qc, kc, vc, sink):
    B, L, Hq, dh = qc.shape
    Hkv = kc.shape[2]
    G = Hq // Hkv
    s = jnp.einsum('bqhgd,bkhd->bhgqk', qc.reshape(B, L, Hkv, G, dh), kc).astype(jnp.float32) * dh ** -0.5
    if sink is None:
        p = jax.nn.softmax(s, axis=-1)
    else:
        s_sink = jnp.broadcast_to(sink.astype(jnp.float32).reshape(1, Hkv, G, 1, 1), s.shape[:-1] + (1,))
        p = jax.nn.softmax(jnp.concatenate([s, s_sink], axis=-1), axis=-1)[..., :-1]
    o = jnp.einsum('bhgqk,bkhd->bqhgd', p.astype(vc.dtype), vc)
    return o.reshape(B, L, Hq * dh)


def neighborhood_attention(q, k, v, kc, vc, rpb):
    B, T, H, dh = q.shape
    rows = T // GRID_W
    kh = min(NA_KH, rows)
    ncb = GRID_W // NA_QW
    r = np.arange(rows)
    row_idx = np.clip(r - kh // 2, 0, rows - kh)[:, None] + np.arange(kh)[None, :]
    q_cols = np.arange(GRID_W).reshape(ncb, NA_QW)
    win_start = np.clip(q_cols - NA_KW // 2, 0, GRID_W - NA_KW)
    col_idx = np.clip(np.arange(ncb) * NA_QW - NA_KW // 2, 0, GRID_W - NA_CB)[:, None] + np.arange(NA_CB)[None, :]
    key_col = col_idx[:, None, :]
    valid = (key_col >= win_start[:, :, None]) & (key_col < win_start[:, :, None] + NA_KW)
    dr = row_idx - r[:, None] + NA_KH - 1
    dc = np.clip(key_col - q_cols[:, :, None] + NA_KW - 1, 0, 2 * NA_KW - 2)
    bias = rpb.astype(jnp.float32)[:, dr[:, None, None, :, None], dc[None, :, :, None, :]]
    bias = jnp.where(valid[None, None, :, :, None, :], bias, -jnp.inf)

    q_blk = q.reshape(B, rows, ncb, NA_QW, H, dh)
    k_grid = k.reshape(B, rows, GRID_W, H, dh)
    v_grid = v.reshape(B, rows, GRID_W, H, dh)
    ri = row_idx[:, None, :, None]
    ci = col_idx[None, :, None, :]
    k_blk = k_grid[:, ri, ci]
    v_blk = v_grid[:, ri, ci]
    scale = dh ** -0.5
    s_loc = jnp.einsum('brnqhd,brnkchd->bhrnqkc', q_blk, k_blk).astype(jnp.float32) * scale + bias
    n_loc = kh * NA_CB
    s_loc = s_loc.reshape(s_loc.shape[:5] + (n_loc,))
    s_ctx = jnp.einsum('brnqhd,blhd->bhrnql', q_blk, kc).astype(jnp.float32) * scale
    p = jax.nn.softmax(jnp.concatenate([s_loc, s_ctx], axis=-1), axis=-1)
    p_loc = p[..., :n_loc].reshape(p.shape[:5] + (kh, NA_CB)).astype(v.dtype)
    p_ctx = p[..., n_loc:].astype(vc.dtype)
    o = (jnp.einsum('bhrnqkc,brnkchd->brnqhd', p_loc, v_blk)
         + jnp.einsum('bhrnql,blhd->brnqhd', p_ctx, vc))
    return o.reshape(B, T, H * dh)


def sliding_window_attention(q, k, v, kc, vc, sink):
    B, T, Hq, dh = q.shape
    Hkv = k.shape[2]
    G = Hq // Hkv
    bs = SWA_BLOCK
    nb = T // bs
    qb = q.reshape(B, nb, bs, Hkv, G, dh)
    pad = ((0, 0), (bs, bs), (0, 0), (0, 0))
    kp, vp = jnp.pad(k, pad), jnp.pad(v, pad)
    idx = np.arange(nb)[:, None] * bs + np.arange(3 * bs)[None, :]
    kb, vb = kp[:, idx], vp[:, idx]
    kpos = idx - bs
    qpos = np.arange(T).reshape(nb, bs)
    valid = ((np.abs(qpos[:, :, None] - kpos[:, None, :]) <= SWA_WINDOW)
             & (kpos[:, None, :] >= 0) & (kpos[:, None, :] < T))
    scale = dh ** -0.5
    s_loc = jnp.einsum('bnqhgd,bnkhd->bhgnqk', qb, kb).astype(jnp.float32) * scale
    s_loc = jnp.where(valid, s_loc, -jnp.inf)
    s_ctx = jnp.einsum('bnqhgd,blhd->bhgnql', qb, kc).astype(jnp.float32) * scale
    s_sink = jnp.broadcast_to(sink.astype(jnp.float32).reshape(1, Hkv, G, 1, 1, 1), s_loc.shape[:-1] + (1,))
    p = jax.nn.softmax(jnp.concatenate([s_loc, s_ctx, s_sink], axis=-1), axis=-1)
    n_loc = 3 * bs
    L = kc.shape[1]
    p_loc = p[..., :n_loc].astype(v.dtype)
    p_ctx = p[..., n_loc:n_loc + L].astype(vc.dtype)
    o = (jnp.einsum('bhgnqk,bnkhd->bnqhgd', p_loc, vb)
         + jnp.einsum('bhgnql,blhd->bnqhgd', p_ctx, vc))
    return o.reshape(B, T, Hq * dh)


def token_mixers(h, hc, w_in, w_out, wa2_f, ba_f, wa2_b, ba_b, gla_norm, na_rpb, swa_sink, cos, sin, need_ctx_out):
    p = split_cols(h @ w_in)
    pc = split_cols(hc @ w_in)
    y_gla, yc_gla = gla_mixer(p[0:6], pc[0:6], wa2_f, ba_f, wa2_b, ba_b, gla_norm, need_ctx_out)
    nq, nk, nv = (split_heads(a, NA_HEADS) for a in p[6:9])
    nkc, nvc = split_heads(pc[7], NA_HEADS), split_heads(pc[8], NA_HEADS)
    y_na = neighborhood_attention(nq, nk, nv, nkc, nvc, na_rpb)
    sq = apply_rope(split_heads(p[9], SWA_HEADS), cos, sin)
    sk = apply_rope(split_heads(p[10], SWA_KV_HEADS), cos, sin)
    sv = split_heads(p[11], SWA_KV_HEADS)
    skc, svc = split_heads(pc[10], SWA_KV_HEADS), split_heads(pc[11], SWA_KV_HEADS)
    y_swa = sliding_window_attention(sq, sk, sv, skc, svc, swa_sink)
    y = jnp.concatenate([y_gla, y_na, y_swa], axis=-1) @ w_out
    if not need_ctx_out:
        return y, None
    yc_na = context_attention(split_heads(pc[6], NA_HEADS), nkc, nvc, None)
    yc_swa = context_attention(split_heads(pc[9], SWA_HEADS), skc, svc, swa_sink)
    yc = jnp.concatenate([yc_gla, yc_na, yc_swa], axis=-1) @ w_out
    return y, yc


def swiglu(h, w_gate, w_up, w_down):
    return (jax.nn.silu(h @ w_gate) * (h @ w_up)) @ w_down


def setup_inputs(seed: int = 0) -> dict:
    key = jax.random.key(seed)
    ks = jax.random.split(key, 24)
    f32 = jnp.float32
    D = D_MODEL

    def nrm(k, shape, s):
        return jax.random.normal(k, shape, f32) * s

    return {
        'x': nrm(ks[0], (BATCH, SEQ, D), 1.0),
        'c': nrm(ks[1], (BATCH, D), 1.0),
        'ctx': nrm(ks[2], (BATCH, CTX_LEN, D), 1.0),
        'c_ctx': nrm(ks[3], (D,), 1.0),
        'w_mod': nrm(ks[4], (DEPTH, D, 6 * D), 0.5 * D ** -0.5),
        'b_mod': nrm(ks[5], (DEPTH, 6 * D), 0.02),
        'norm_mix': 1.0 + nrm(ks[6], (DEPTH, D), 0.05),
        'norm_ffn': 1.0 + nrm(ks[7], (DEPTH, D), 0.05),
        'w_in': nrm(ks[8], (DEPTH, D, IN_WIDTH), D ** -0.5),
        'gla_wa2_f': nrm(ks[9], (DEPTH, GLA_RANK, GLA_W), GLA_RANK ** -0.5),
        'gla_ba_f': nrm(ks[10], (DEPTH, GLA_W), 0.5),
        'gla_wa2_b': nrm(ks[11], (DEPTH, GLA_RANK, GLA_W), GLA_RANK ** -0.5),
        'gla_ba_b': nrm(ks[12], (DEPTH, GLA_W), 0.5),
        'gla_norm': 1.0 + nrm(ks[13], (DEPTH, GLA_W), 0.05),
        'na_rpb': nrm(ks[14], (DEPTH, NA_HEADS, 2 * NA_KH - 1, 2 * NA_KW - 1), 0.1),
        'swa_sink': nrm(ks[15], (DEPTH, SWA_HEADS), 0.5),
        'w_out': nrm(ks[16], (DEPTH, MIX_WIDTH, D), MIX_WIDTH ** -0.5),
        'w_gate': nrm(ks[17], (DEPTH, D, D_FF), D ** -0.5),
        'w_up': nrm(ks[18], (DEPTH, D, D_FF), D ** -0.5),
        'w_down': nrm(ks[19], (DEPTH, D_FF, D), D_FF ** -0.5),
        'final_norm': 1.0 + nrm(ks[20], (D,), 0.05),
    }


def reference(x, c, ctx, c_ctx, w_mod, b_mod, norm_mix, norm_ffn, w_in, gla_wa2_f, gla_ba_f, gla_wa2_b, gla_ba_b,
              gla_norm, na_rpb, swa_sink, w_out, w_gate, w_up, w_down, final_norm):
    T = x.shape[1]
    cos, sin = axial_rope(T)
    s_lat = jax.nn.silu(c)
    s_ctx = jax.nn.silu(c_ctx)
    xc = ctx
    for i in range(DEPTH):
        need_ctx_out = i < DEPTH - 1
        mod = (s_lat @ w_mod[i] + b_mod[i])[:, None, :]
        mod_c = s_ctx @ w_mod[i] + b_mod[i]
        sh1, sc1, g1, sh2, sc2, g2 = jnp.split(mod, 6, axis=-1)
        sh1c, sc1c, g1c, sh2c, sc2c, g2c = jnp.split(mod_c, 6, axis=-1)
        h = rms_norm(x, norm_mix[i]) * (1.0 + sc1) + sh1
        hc = rms_norm(xc, norm_mix[i]) * (1.0 + sc1c) + sh1c
        y, yc = token_mixers(h, hc, w_in[i], w_out[i], gla_wa2_f[i], gla_ba_f[i], gla_wa2_b[i], gla_ba_b[i],
                             gla_norm[i], na_rpb[i], swa_sink[i], cos, sin, need_ctx_out)
        x = x + g1 * y
        h = rms_norm(x, norm_ffn[i]) * (1.0 + sc2) + sh2
        x = x + g2 * swiglu(h, w_gate[i], w_up[i], w_down[i])
        if need_ctx_out:
            xc = xc + g1c * yc
            hc = rms_norm(xc, norm_ffn[i]) * (1.0 + sc2c) + sh2c
            xc = xc + g2c * swiglu(hc, w_gate[i], w_up[i], w_down[i])
    return rms_norm(x, final_norm)
```

```python
import numpy as np
import ml_dtypes
from contextlib import ExitStack
import concourse.bass as bass
import concourse.mybir as mybir
from concourse.bass_utils import run_bass_kernel_spmd

F32 = mybir.dt.float32
BF16 = mybir.dt.bfloat16
ALU = mybir.AluOpType
AF = mybir.ActivationFunctionType

D = 1024
SEQ = 4096
NT = 2048
LC = 256
NTK = NT + LC
DFF = 2816
NFB = DFF // 128
DEPTH = 2
EPS = 1e-6
NEG = -30000.0
NSLOT = 3
WSL = 1408
TT = [(0, 512), (512, 512), (1024, 512), (1536, 512), (2048, 256)]
SAME_ENGINE_SYNC = True


class Ev:
    __slots__ = ("sem", "val", "eng")

    def __init__(self, sem, val, eng):
        self.sem, self.val, self.eng = sem, val, eng


class Tk:
    __slots__ = ("name", "w", "r", "base", "fence")
    cur_fence = ()

    def __init__(self, name):
        self.name, self.w, self.r, self.base = name, {}, {}, None
        self.fence = Tk.cur_fence


class Eng:
    def __init__(self, b, name, e, is_pe=False):
        self.b, self.name, self.e, self.is_pe = b, name, e, is_pe
        self.sem = b.newsem(name)
        self.count = 0
        self.waited = {}
        self.pending = []
        self.nwait = 0
        self.ninst = 0

    def wait(self, ev):
        if ev is None:
            return
        if ev.eng is self and (self.is_pe or not SAME_ENGINE_SYNC):
            return
        assert ev.val is not None, f"wait on unsignaled access (eng {ev.eng.name}) from {self.name}"
        key = id(ev.sem)
        if self.waited.get(key, 0) >= ev.val:
            return
        self.e.wait_ge(ev.sem, ev.val)
        self.nwait += 1
        self.waited[key] = ev.val

    def signal(self, ins):
        if self.count >= 30000:
            self.sem = self.b.newsem(self.name)
            self.count = 0
        self.count += 1
        ins.then_inc(self.sem, 1)
        ev = Ev(self.sem, self.count, self)
        for p in self.pending:
            p.sem, p.val = ev.sem, ev.val
        self.pending = []
        return ev

    def defer(self):
        ev = Ev(None, None, self)
        self.pending.append(ev)
        return ev


class Builder:
    def __init__(self):
        self.nc = bass.Bass("TRN2", target_bir_lowering=False)
        self.es = ExitStack()
        self.nsem = 0
        nc = self.nc
        self.engs = {
            "pe": Eng(self, "pe", nc.tensor, is_pe=True),
            "act": Eng(self, "act", nc.scalar),
            "dve": Eng(self, "dve", nc.vector),
            "pool": Eng(self, "pool", nc.gpsimd),
            "sp": Eng(self, "sp", nc.sync),
        }
        self.dma_sems = []
        for i in range(12):
            self.dma_sems.append([self.newsem(f"dma{i}"), 0, None])
        self.dma_rr = 0
        self.din = {}
        self.dout = {}

    def newsem(self, name):
        self.nsem += 1
        return self.es.enter_context(self.nc.semaphore(f"{name}_{self.nsem}"))

    def sb(self, name, shape, dt, stack=None):
        self.nsb = getattr(self, "nsb", 0) + 1
        return (stack or self.es).enter_context(self.nc.sbuf_tensor(f"s{self.nsb}_{name}", shape, dt))

    def inp(self, name, shape, dt=F32):
        t = self.nc.dram_tensor(name, list(shape), dt, kind="ExternalInput").ap()
        self.din[name] = t
        return t

    def outp(self, name, shape, dt=F32):
        t = self.nc.dram_tensor(name, list(shape), dt, kind="ExternalOutput").ap()
        self.dout[name] = t
        return t

    def fence(self):
        evs = []
        for E in self.engs.values():
            assert not E.pending, f"fence with unsignaled {E.name} work"
            if E.count:
                evs.append(Ev(E.sem, E.count, E))
        for slot in self.dma_sems:
            if slot[2] is not None:
                evs.append(slot[2])
        Tk.cur_fence = tuple(evs)

    def scope(self):
        b = self

        class _S(ExitStack):
            def __exit__(self, *a):
                r = super().__exit__(*a)
                if a[0] is None:
                    b.fence()
                return r
        return _S()

    def _deps(self, E, rd, wr, pw):
        for lst in (rd, wr, pw):
            for t in lst:
                for ev in t.fence:
                    E.wait(ev)
        for t in rd:
            for ev in t.w.values():
                E.wait(ev)
        for t in wr:
            for ev in t.w.values():
                E.wait(ev)
            for ev in t.r.values():
                E.wait(ev)
        for t in pw:
            E.wait(t.base)
            for ev in t.r.values():
                E.wait(ev)

    def _mark(self, key, ev, rd, wr, pw):
        for t in rd:
            t.r[key] = ev
        for t in wr:
            t.w = {key: ev}
            t.base = ev
            t.r = {}
        for t in pw:
            t.w[key] = ev

    def op(self, eng, fn, rd=(), wr=(), pw=(), sig=True):
        E = self.engs[eng]
        self._deps(E, rd, wr, pw)
        ins = fn(E.e)
        E.ninst += 1
        ev = E.signal(ins) if sig else E.defer()
        self._mark(E.name, ev, rd, wr, pw)
        return ev

    def dma(self, out, in_, rd=(), wr=(), pw=(), q="sp"):
        E = self.engs[q]
        self._deps(E, rd, wr, pw)
        k = self.dma_rr
        self.dma_rr = (self.dma_rr + 1) % len(self.dma_sems)
        slot = self.dma_sems[k]
        if slot[2] is not None:
            E.wait(slot[2])
        if slot[1] >= 30000:
            slot[0] = self.newsem(f"dma{k}")
            slot[1] = 0
        slot[1] += 16
        E.e.dma_start(out=out, in_=in_).then_inc(slot[0], 16)
        E.ninst += 1
        ev = Ev(slot[0], slot[1], None)
        slot[2] = ev
        self._mark(f"dma{k}", ev, rd, wr, pw)
        return ev


def _blk(W, cols):
    K = W.shape[0]
    Wc = W[:, np.asarray(cols)]
    return np.ascontiguousarray(Wc.reshape(K // 128, 128, len(cols)).transpose(1, 0, 2))


def _vec(v, nb):
    return np.ascontiguousarray(v.reshape(nb, 128).T)


C_GQ, C_GK, C_GV, C_GG, C_ZF, C_ZB = 0, 256, 512, 768, 1024, 1040
C_NQ, C_NK, C_NV = 1056, 1440, 1824
C_SQ, C_SK, C_SV = 2208, 2592, 2720
SWA_BLOCK_HEADS = [(0, 3), (1, 4), (2, 5)]


def _win_blocks(w_in_l):
    r = np.arange
    blocks = []
    for pr in range(2):
        for base in (C_GQ, C_GK, C_GG, C_GV):
            blocks.append(_blk(w_in_l, base + pr * 128 + r(128)))
    for base in (C_NQ, C_NK, C_NV):
        for j in range(3):
            blocks.append(_blk(w_in_l, base + j * 128 + r(128)))
    perm = np.concatenate([r(32, 64), r(0, 32)])
    for use_perm in (False, True):
        for (ha, hb) in SWA_BLOCK_HEADS:
            idx = perm if use_perm else r(64)
            blocks.append(_blk(w_in_l, np.concatenate([C_SQ + 64 * ha + idx, C_SQ + 64 * hb + idx])))
    blocks.append(_blk(w_in_l, C_SK + r(128)))
    blocks.append(_blk(w_in_l, np.concatenate([C_SK + perm, C_SK + 64 + perm])))
    blocks.append(_blk(w_in_l, C_SV + r(128)))
    return np.stack(blocks).reshape(len(blocks), 128, 1024)


B_GQ, B_GK, B_GG, B_GV = 0, 1, 2, 3
B_NQ, B_NK, B_NV = 8, 11, 14
B_SQ, B_SQP, B_SK, B_SKP, B_SV = 17, 20, 23, 24, 25
NWIN = 26


def _yrows():
    rows = list(range(640))
    for (ha, hb) in SWA_BLOCK_HEADS:
        rows += list(range(640 + 64 * ha, 640 + 64 * ha + 64)) + list(range(640 + 64 * hb, 640 + 64 * hb + 64))
    return np.asarray(rows)


def _na_table(rpb_h, par):
    out = np.full((128, 2048), NEG, np.float32)
    kro = np.arange(128) // 64
    kcl = np.arange(128) % 64

    def fill(col0, dloc, kc_l, c_l, interior):
        if par == 0:
            dg, kc, c = dloc, kc_l, c_l
        else:
            dg, kc, c = -dloc, 63 - kc_l, 63 - c_l
        ws = np.clip(c - 8, 0, 48)
        ok = (kc >= ws) & (kc < ws + 16)
        if interior:
            ok = ok & (dg >= -4) & (dg <= 3)
        ok = ok & (dg + 7 >= 0) & (dg + 7 <= 14)
        dr = np.clip(dg + 7, 0, 14)
        dc = np.clip(kc - c + 15, 0, 30)
        vals = rpb_h[dr, dc]
        blk = np.where(ok, vals, np.float32(NEG)).astype(np.float32)
        out[:, col0:col0 + blk.shape[1]] = blk

    c64 = np.arange(64)[None, :]
    for u in range(14):
        fill(u * 64, (6 - u) + kro[:, None] + 0 * c64, kcl[:, None] + 0 * c64, c64 + 0 * kcl[:, None], True)
    for u in range(10):
        fill(896 + u * 64, (6 - u) + kro[:, None] + 0 * c64, kcl[:, None] + 0 * c64, c64 + 0 * kcl[:, None], False)
    for jj in range(2):
        for qr in range(4):
            krl = 32 + 2 * jj + (1 - kro)
            dloc = (krl - (28 + qr))[:, None] + 0 * c64
            fill(1536 + jj * 256 + qr * 64, dloc, (63 - kcl)[:, None] + 0 * c64, c64 + 0 * kcl[:, None], True)
    return out


def _rope_tables(par):
    i = np.arange(NT)
    t = i if par == 0 else (SEQ - 1 - i)
    row = (t // 64).astype(np.float32)
    col = (t % 64).astype(np.float32)
    inv = (np.float32(10000.0) ** (-np.arange(16, dtype=np.float32) / np.float32(16))).astype(np.float32)
    ang = np.concatenate([row[:, None] * inv, col[:, None] * inv], axis=-1).astype(np.float32)
    cs, sn = np.cos(ang).astype(np.float32), np.sin(ang).astype(np.float32)
    p = np.arange(128) % 64
    f = p % 32
    cosT = np.ascontiguousarray(cs[:, f].T)
    sinT = np.ascontiguousarray(sn[:, f].T) * np.where(p < 32, -1.0, 1.0).astype(np.float32)[:, None]
    return cosT.astype(np.float32), sinT.astype(np.float32)


def _consts():
    bf = ml_dtypes.bfloat16
    c = {}
    c["identf"] = np.eye(128, dtype=np.float32)
    c["identb"] = np.eye(128, dtype=np.float32).astype(bf)
    c["onesb"] = np.ones((128, 128), np.float32).astype(bf)
    bd = np.zeros((128, 128), np.float32)
    bd[:64, :64] = 1.0 / 64
    bd[64:, 64:] = 1.0 / 64
    c["bdiag"] = bd.astype(bf)
    s = (np.arange(128) % 64)[:, None]
    cc = np.arange(64)[None, :]
    m = np.zeros((128, 2, 2, 64), np.float32)
    m[:, :, 0, :] = (s <= cc)[:, None, :]
    m[:, :, 1, :] = (s >= cc)[:, None, :]
    c["glamask"] = m.reshape(128, 256).astype(bf)
    ki = np.arange(128)[:, None]
    qi = np.arange(128)[None, :]
    big = np.float32(NEG * 8)
    kinds = [ki >= qi, ki <= qi, (ki + qi) >= 127]
    sm = np.stack([np.tile(np.where(k, np.float32(0), big), (1, 3)) for k in kinds])
    c["swamask"] = np.ascontiguousarray(sm.transpose(1, 0, 2).reshape(128, 3 * 384)).astype(bf)
    return c


def prep_inputs(inp):
    f32 = np.float32
    g = {k: np.asarray(v) for k, v in inp.items()}
    shared = {}
    wmod_all = np.stack([np.stack([_blk(g["w_mod"][l], fb * 128 + np.arange(128)).reshape(128, 1024)
                                   for fb in range(48)]) for l in range(DEPTH)])
    bmod_all = np.stack([_vec(g["b_mod"][l], 48) for l in range(DEPTH)])
    shared["nmix"] = np.stack([_vec(g["norm_mix"][l], 8) for l in range(DEPTH)])
    shared["nffn"] = np.stack([_vec(g["norm_ffn"][l], 8) for l in range(DEPTH)])
    shared["fnorm"] = _vec(g["final_norm"], 8)
    shared["win"] = np.stack([_win_blocks(g["w_in"][l]) for l in range(DEPTH)])
    yr = _yrows()
    shared["wout"] = np.stack([np.stack([_blk(g["w_out"][l][yr], ob * 128 + np.arange(128)).reshape(128, 1024)
                                         for ob in range(8)]) for l in range(DEPTH)])
    shared["wgu"] = np.stack([np.stack([np.stack([_blk(g[nm][l], fb * 128 + np.arange(128)).reshape(128, 1024)
                                                  for nm in ("w_gate", "w_up")]) for fb in range(NFB)])
                              for l in range(DEPTH)])
    wdn = []
    for l in range(DEPTH):
        obs = []
        for ob in range(8):
            b = _blk(g["w_down"][l], ob * 128 + np.arange(128))
            obs.append(np.stack([b[:, 0:11].reshape(128, 1408), b[:, 11:22].reshape(128, 1408)]))
        wdn.append(np.stack(obs))
    shared["wdn"] = np.stack(wdn)
    shared["gnorm"] = np.stack([_vec(g["gla_norm"][l], 2) for l in range(DEPTH)])
    shared["sinkb"] = np.stack([np.tile(g["swa_sink"][l][None, :], (128, 1)) for l in range(DEPTH)]).astype(f32)
    shared.update(_consts())
    perpar = []
    for par in range(2):
        d = {}
        zf = [np.arange(C_ZF, C_ZF + 16), np.arange(C_ZB, C_ZB + 16)]
        wa = [g["gla_wa2_f"], g["gla_wa2_b"]]
        ba = [g["gla_ba_f"], g["gla_ba_b"]]
        lf, lb = (0, 1) if par == 0 else (1, 0)
        zl, wa2, gba = [], [], []
        for l in range(DEPTH):
            z = np.zeros((128, 8, 48), f32)
            z[:, :, 0:16] = _blk(g["w_in"][l], zf[lf])
            z[:, :, 32:48] = _blk(g["w_in"][l], zf[lb])
            zl.append(z.reshape(128, 384))
            w = np.zeros((48, 256), f32)
            w[0:16] = wa[lf][l]
            w[32:48] = wa[lb][l]
            wa2.append(w)
            b = np.zeros((128, 2, 2), f32)
            b[:, :, 0] = _vec(ba[lf][l], 2)
            b[:, :, 1] = _vec(ba[lb][l], 2)
            gba.append(b.reshape(128, 4))
        d["zl"], d["wa2"], d["gba"] = np.stack(zl), np.stack(wa2), np.stack(gba)
        d["natab"] = np.stack([np.stack([_na_table(g["na_rpb"][l][h], par) for h in range(6)]) for l in range(DEPTH)])
        d["cosT"], d["sinT"] = _rope_tables(par)
        sel = np.zeros((128, 2), f32)
        sel[:, 1 - par] = 1.0
        d["sel"] = sel
        d["wmodh"] = np.ascontiguousarray(wmod_all[:, par * 24:(par + 1) * 24])
        d["bmodh"] = np.ascontiguousarray(bmod_all[:, :, par * 24:(par + 1) * 24])
        perpar.append(d)
    maps = []
    for c in range(8):
        b, par = c // 2, c % 2
        m = dict(shared)
        m.update(perpar[par])
        if par == 0:
            m["x"] = np.ascontiguousarray(g["x"][b, 0:NT])
            m["ctx"] = np.ascontiguousarray(g["ctx"][b])
        else:
            m["x"] = np.ascontiguousarray(g["x"][b, NT:SEQ][::-1])
            m["ctx"] = np.ascontiguousarray(g["ctx"][b][::-1])
        cc = np.zeros((128, 8, 2), f32)
        cc[:, :, 0] = _vec(g["c"][b], 8)
        cc[:, :, 1] = _vec(g["c_ctx"], 8)
        m["cc"] = cc.reshape(128, 16)
        maps.append(m)
    return maps


class _Stop(Exception):
    pass


class Prog(Builder):
    DSHAPES = {
        "x": ([NT, D], F32),
        "ctx": ([LC, D], F32),
        "cc": ([128, 16], F32),
        "sel": ([128, 2], F32),
        "wmodh": ([DEPTH, 24, 128, 1024], F32),
        "bmodh": ([DEPTH, 128, 24], F32),
        "nmix": ([DEPTH, 128, 8], F32),
        "nffn": ([DEPTH, 128, 8], F32),
        "fnorm": ([128, 8], F32),
        "win": ([DEPTH, NWIN, 128, 1024], F32),
        "zl": ([DEPTH, 128, 384], F32),
        "wa2": ([DEPTH, 48, 256], F32),
        "gba": ([DEPTH, 128, 4], F32),
        "gnorm": ([DEPTH, 128, 2], F32),
        "natab": ([DEPTH, 6, 128, 2048], F32),
        "sinkb": ([DEPTH, 128, 6], F32),
        "wout": ([DEPTH, 8, 128, 1024], F32),
        "wgu": ([DEPTH, NFB, 2, 128, 1024], F32),
        "wdn": ([DEPTH, 8, 2, 128, 1408], F32),
        "cosT": ([128, NT], F32),
        "sinT": ([128, NT], F32),
        "identf": ([128, 128], F32),
        "identb": ([128, 128], BF16),
        "onesb": ([128, 128], BF16),
        "bdiag": ([128, 128], BF16),
        "glamask": ([128, 256], BF16),
        "swamask": ([128, 1152], BF16),
    }

    def __getattr__(self, name):
        if name.startswith("d_") and name[2:] in Prog.DSHAPES:
            sh, dt = Prog.DSHAPES[name[2:]]
            t = self.inp(name[2:], sh, dt)
            setattr(self, name, t)
            return t
        raise AttributeError(name)
    def __init__(self, upto="all", dbg=()):
        super().__init__()
        self.upto = upto
        self.dbgnames = set(dbg)
        self.out_evs = []
        self.wi = 0
        self.wbi = 0
        self.pbi = 0

    def setup(self):
        nc = self.nc
        self.d_y = self.outp("y", [NT, D])
        self.d_xsave = nc.dram_tensor("xsave", [128, 8 * NTK], F32).ap()

        sb = self.sb
        self.HT = sb("HT", [128, 8, NTK], BF16)
        self.YT = sb("YT", [128, 8, NTK], BF16)
        self.WF = sb("WF", [128, NSLOT, WSL], F32)
        self.WB = sb("WB", [128, NSLOT, WSL], BF16)
        self.wf_tk = [Tk(f"wf{i}") for i in range(NSLOT)]
        self.wb_tk = [Tk(f"wb{i}") for i in range(NSLOT)]
        self.hT_tk = [[Tk(f"hT{k}_{t}") for t in range(5)] for k in range(8)]
        self.yT_tk = [[Tk(f"yT{k}_{t}") for t in range(5)] for k in range(8)]
        self.xT_tk = [[Tk(f"xT{k}_{t}") for t in range(5)] for k in range(8)]
        self.identf = sb("identf", [128, 128], F32)
        self.identb = sb("identb", [128, 128], BF16)
        self.onesb = sb("onesb", [128, 128], BF16)
        self.bdiag = sb("bdiag", [128, 128], BF16)
        self.glamask = sb("glamask", [128, 256], BF16)
        self.swamask = sb("swamask", [128, 1152], BF16)
        self.sel = sb("sel", [128, 2], F32)
        self.cc = sb("cc", [128, 16], F32)
        self.sT = sb("sT", [128, 16], F32)
        self.bmod = sb("bmodh", [128, DEPTH, 24], F32)
        self.modH = sb("modH", [128, 24, 2], F32)
        self.k_modh = Tk("modH")
        self.nmix = sb("nmix", [128, DEPTH, 8], F32)
        self.nffn = sb("nffn", [128, DEPTH, 8], F32)
        self.fnorm = sb("fnorm", [128, 8], F32)
        self.gba = sb("gba", [128, DEPTH, 4], F32)
        self.ngba = sb("ngba", [128, DEPTH, 4], F32)
        self.gnorm = sb("gnorm", [128, DEPTH, 2], F32)
        self.sinkb = sb("sinkb", [128, DEPTH, 6], F32)
        self.esink = sb("esink", [128, DEPTH, 6], F32)
        self.wa2 = sb("wa2", [48, DEPTH, 256], F32)
        self.modT = sb("modT", [128, 48, 2], F32)
        self.A1 = sb("A1", [128, 8, 2], F32)
        self.A2 = sb("A2", [128, 8, 2], F32)
        self.epsT = sb("epsT", [128, 1], F32)
        self.k_const = Tk("const")
        self.k_mod = Tk("mod")
        self.k_s = Tk("sT")
        self.pb = [self.es.enter_context(nc.psum_tensor(f"pb{i}", [128, 512], F32)) for i in range(6)]
        self.pb_tk = [Tk(f"pb{i}") for i in range(6)]
        self.pbt = [self.es.enter_context(nc.psum_tensor(f"pbt{i}", [128, 1024], BF16)) for i in range(2)]
        self.pbt_tk = [Tk("pbt0"), Tk("pbt1")]
        self.onesf = sb("onesf", [128, 1], F32)

        K = self.k_const
        ld = lambda dst, src: self.dma(out=dst, in_=src, pw=[K])
        ld(self.identf[:], self.d_identf[:, :])
        ld(self.identb[:], self.d_identb[:, :])
        ld(self.onesb[:], self.d_onesb[:, :])
        ld(self.bdiag[:], self.d_bdiag[:, :])
        ld(self.glamask[:], self.d_glamask[:, :])
        ld(self.swamask[:], self.d_swamask[:, :])
        ld(self.sel[:], self.d_sel[:, :])
        ld(self.cc[:], self.d_cc[:, :])
        ld(self.fnorm[:], self.d_fnorm[:, :])
        for l in range(DEPTH):
            ld(self.bmod[:, l, :], self.d_bmodh[l, :, :])
            ld(self.nmix[:, l, :], self.d_nmix[l, :, :])
            ld(self.nffn[:, l, :], self.d_nffn[l, :, :])
            ld(self.gba[:, l, :], self.d_gba[l, :, :])
            ld(self.gnorm[:, l, :], self.d_gnorm[l, :, :])
            ld(self.sinkb[:, l, :], self.d_sinkb[l, :, :])
            ld(self.wa2[:, l, :], self.d_wa2[l, :, :])
        self.op("act", lambda e: e.activation(out=self.sT[:], in_=self.cc[:], func=AF.Silu), rd=[K], wr=[self.k_s])
        self.op("act", lambda e: e.activation(out=self.esink[:], in_=self.sinkb[:], func=AF.Exp), rd=[K], pw=[K])
        self.op("dve", lambda e: e.tensor_scalar(out=self.ngba[:], in0=self.gba[:], scalar1=-1.0, scalar2=None,
                                                 op0=ALU.mult), rd=[K], pw=[K])
        self.op("dve", lambda e: e.memset(self.epsT[:], EPS), pw=[K])
        self.op("dve", lambda e: e.memset(self.onesf[:], 1.0), pw=[K])

    def stop_at(self, name):
        if self.upto == name:
            raise _Stop()

    def bank(self, i=None, lo=0, hi=6):
        if i is None:
            key = (lo, hi)
            self.pbrr = getattr(self, "pbrr", {})
            c = self.pbrr.get(key, 0)
            self.pbrr[key] = c + 1
            i = lo + c % (hi - lo)
        return self.pb[i], self.pb_tk[i]

    def wload(self, src, n, cast=True):
        s = self.wi % NSLOT
        self.wi += 1
        self.dma(out=self.WF[:, s, 0:n], in_=src, wr=[self.wf_tk[s]])
        if not cast:
            return self.WF[:, s, 0:n], self.wf_tk[s]
        b = self.wbi % NSLOT
        self.wbi += 1
        self.op("pool", lambda e: e.tensor_copy(out=self.WB[:, b, 0:n], in_=self.WF[:, s, 0:n]),
                rd=[self.wf_tk[s]], wr=[self.wb_tk[b]])
        return self.WB[:, b, 0:n], self.wb_tk[b]

    def mm(self, pk, out, lhsT, rhs, first, last, rd, sig=None, **kw):
        return self.op("pe", lambda e: e.matmul(out, lhsT=lhsT, rhs=rhs, start=first, stop=last, **kw),
                       rd=rd, wr=[pk] if first else (), pw=() if first else [pk],
                       sig=last if sig is None else sig)

    def dump(self, name, ap, shape, dt, rd):
        if name not in self.dbgnames:
            return
        o = self.outp("dbg_" + name, shape, dt)
        self.out_evs.append(self.dma(out=o, in_=ap, rd=rd))

    def finish(self):
        E = self.engs["sp"]
        for ev in self.out_evs:
            E.wait(ev)

    def phase_load(self, xT):
        with self.scope() as st:
            xin = self.sb("xin", [128, 2, D], F32, st)
            xin_tk = [Tk("xin0"), Tk("xin1")]
            for i in range(18):
                src = self.d_x[i * 128:(i + 1) * 128, :] if i < 16 else self.d_ctx[(i - 16) * 128:(i - 15) * 128, :]
                s = i % 2
                t = i // 4
                self.dma(out=xin[:, s, :], in_=src, wr=[xin_tk[s]])
                for half in range(2):
                    pb, pk = self.bank()
                    for j in range(4):
                        kk = half * 4 + j
                        self.op("pe", lambda e, j=j, kk=kk: e.transpose(out=pb[:, j * 128:(j + 1) * 128],
                                                                        in_=xin[:, s, kk * 128:(kk + 1) * 128],
                                                                        identity=self.identf[:]),
                                rd=[xin_tk[s], self.k_const], wr=[pk] if j == 0 else (), pw=[pk] if j else (),
                                sig=(j == 3))
                    dst = xT[:, half * 4:half * 4 + 4, i * 128:(i + 1) * 128]
                    srcp = pb[:, 0:512].rearrange("p (a b) -> p a b", a=4)
                    tks = [self.xT_tk[half * 4 + j][t] for j in range(4)]
                    if half == 0:
                        self.op("dve", lambda e: e.tensor_copy(out=dst, in_=srcp), rd=[pk], pw=tks)
                    else:
                        self.op("act", lambda e: e.copy(out=dst, in_=srcp), rd=[pk], pw=tks)

    def phase_mod(self, l):
        pb, pk = self.bank()
        for fb in range(24):
            w, wk = self.wload(self.d_wmodh[l, fb, :, :], 1024, cast=False)
            for kc in range(8):
                self.mm(pk, pb[:, fb * 2:fb * 2 + 2], w[:, kc * 128:(kc + 1) * 128], self.sT[:, kc * 2:kc * 2 + 2],
                        kc == 0, kc == 7, rd=[wk, self.k_s])
        for w_ in range(2):
            self.op("dve", lambda e, w_=w_: e.tensor_tensor(out=self.modH[:, :, w_], in0=pb[:, w_:48:2],
                                                             in1=self.bmod[:, l, :], op=ALU.add),
                    rd=[pk, self.k_const], wr=[self.k_modh] if w_ == 0 else (), pw=[self.k_modh] if w_ else ())
        rcv, cev = self.exchange(f"mod{l}", [(0, self.modH[:].rearrange("p a b -> p (a b)"))], 48, F32, rd=[self.k_modh])
        tk = Tk("modrcv")
        tk.w = {"cc": cev}
        mflat = self.modT[:].rearrange("p a b -> p (a b)")
        self.dma(out=mflat[:, 0:48], in_=rcv[0:128, :], rd=[tk], wr=[self.k_mod])
        self.dma(out=mflat[:, 48:96], in_=rcv[128:256, :], rd=[tk], pw=[self.k_mod])
        for w_ in range(2):
            self.op("dve", lambda e, w_=w_: e.scalar_tensor_tensor(out=self.A1[:, :, w_], in0=self.modT[:, 8:16, w_],
                                                                    scalar=1.0, in1=self.nmix[:, l, :],
                                                                    op0=ALU.add, op1=ALU.mult),
                    rd=[self.k_mod, self.k_const], pw=[self.k_mod])
            self.op("dve", lambda e, w_=w_: e.scalar_tensor_tensor(out=self.A2[:, :, w_], in0=self.modT[:, 32:40, w_],
                                                                    scalar=1.0, in1=self.nffn[:, l, :],
                                                                    op0=ALU.add, op1=ALU.mult),
                    rd=[self.k_mod, self.k_const], pw=[self.k_mod])

    def phase_norm(self, xT, A, boff, tiles):
        with self.scope() as st:
            sq = self.sb("sq", [128, 2, 4, 512], BF16, st)
            rs = self.sb("rs", [128, 2, 512], F32, st)
            tmp = self.sb("ntmp", [128, 3, 512], F32, st)
            sq_tk = [Tk("sq0"), Tk("sq1")]
            rs_tk = [Tk("rs0"), Tk("rs1")]
            tmp_tk = [Tk(f"ntmp{i}") for i in range(3)]
            ti = 0
            hi = 0
            for it, t in enumerate(tiles):
                t0, n = TT[t]
                wsel = 1 if t == 4 else 0
                s = it % 2
                pb, pk = self.bank()
                for half in range(2):
                    hs = hi % 2
                    hi += 1
                    for j in range(4):
                        k = half * 4 + j
                        self.op("act", lambda e, k=k, j=j: e.activation(out=sq[:, hs, j, 0:n], in_=xT[:, k, t0:t0 + n],
                                                                        func=AF.Square),
                                rd=[self.xT_tk[k][t]], wr=[sq_tk[hs]] if j == 0 else (), pw=[sq_tk[hs]] if j else ())
                    for j in range(4):
                        k = half * 4 + j
                        self.mm(pk, pb[:, 0:n], self.onesb[:], sq[:, hs, j, 0:n], k == 0, k == 7,
                                rd=[sq_tk[hs], self.k_const], sig=(j == 3))
                self.op("act", lambda e, pb=pb: e.activation(out=pb[:, 0:n], in_=pb[:, 0:n], func=AF.Sqrt,
                                                             bias=self.epsT[:, 0:1], scale=1.0 / D),
                        rd=[self.k_const], wr=[pk])
                self.op("dve", lambda e, pb=pb: e.reciprocal(out=pb[:, 0:n], in_=pb[:, 0:n]), wr=[pk])
                for k in range(8):
                    q = ti % 3
                    ti += 1
                    self.op("dve", lambda e, k=k, q=q, pb=pb: e.scalar_tensor_tensor(
                        out=tmp[:, q, 0:n], in0=xT[:, k, t0:t0 + n], scalar=A[:, k, wsel:wsel + 1],
                        in1=pb[:, 0:n], op0=ALU.mult, op1=ALU.mult),
                        rd=[self.xT_tk[k][t], pk, self.k_mod], wr=[tmp_tk[q]])
                    self.op("act", lambda e, k=k, q=q: e.activation(
                        out=self.HT[:, k, t0:t0 + n], in_=tmp[:, q, 0:n], func=AF.Identity,
                        bias=self.modT[:, boff + k, wsel:wsel + 1], scale=1.0),
                        rd=[tmp_tk[q], self.k_mod], wr=[self.hT_tk[k][t]])

    def inproj(self, l, blk, tiles, evac, src=None, ncols=128):
        w, wk = self.wload(self.d_win[l, blk, :, :] if src is None else src, 8 * ncols)
        for t in tiles:
            t0, n = TT[t]
            pb, pk = self.bank()
            for kc in range(8):
                self.mm(pk, pb[0:ncols, 0:n], w[:, kc * ncols:(kc + 1) * ncols], self.HT[:, kc, t0:t0 + n],
                        kc == 0, kc == 7, rd=[wk, self.hT_tk[kc][t]])
            evac(t, t0, n, pb, pk)

    def vproj(self, l, blk, tiles128, evac):
        w, wk = self.wload(self.d_win[l, blk, :, :], 1024)
        for g0 in range(0, len(tiles128), 4):
            grp = tiles128[g0:g0 + 4]
            pb, pk = self.bank()
            for j, i in enumerate(grp):
                for kc in range(8):
                    self.mm(pk, pb[:, j * 128:(j + 1) * 128], self.HT[:, kc, i * 128:(i + 1) * 128],
                            w[:, kc * 128:(kc + 1) * 128], kc == 0, kc == 7, rd=[wk, self.hT_tk[kc][i // 4]])
            evac(grp, pb, pk)

    def exchange(self, name, send_aps, width, dt, rd):
        nc = self.nc
        snd = nc.dram_tensor("snd_" + name, [128, width], dt)
        rcv = nc.dram_tensor("rcv_" + name, [256, width], dt)
        evs = []
        for (c0, ap) in send_aps:
            w_ = ap.shape[-1] if len(ap.shape) == 2 else int(np.prod(ap.shape[1:]))
            evs.append(self.dma(out=snd.ap()[:, c0:c0 + w_], in_=ap, rd=rd))
        E = self.engs["pool"]
        for ev in evs:
            E.wait(ev)
        sem = self.newsem("cc_" + name)
        E.e.collective_compute("AllGather", ALU.bypass, replica_groups=[[0, 1], [2, 3], [4, 5], [6, 7]],
                               ins=[snd.ap().opt()], outs=[rcv.ap().opt()]).then_inc(sem)
        E.ninst += 1
        return rcv.ap(), Ev(sem, 1, None)

    def recv_blend(self, rcv, ev, c0, w_, dst, dst_tks, tmp, tmp_tk, eng="dve", pw=False):
        tk = Tk("rcvev")
        tk.w = {"cc": ev}
        self.dma(out=tmp[:, 0, 0:w_], in_=rcv[0:128, c0:c0 + w_], rd=[tk], wr=[tmp_tk[0]])
        self.dma(out=tmp[:, 1, 0:w_], in_=rcv[128:256, c0:c0 + w_], rd=[tk], wr=[tmp_tk[1]])
        self.op(eng, lambda e: e.tensor_scalar(out=tmp[:, 0, 0:w_], in0=tmp[:, 0, 0:w_], scalar1=self.sel[:, 0:1],
                                               scalar2=None, op0=ALU.mult), rd=[self.k_const], wr=[tmp_tk[0]])
        if eng == "dve":
            self.op("dve", lambda e: e.scalar_tensor_tensor(out=dst, in0=tmp[:, 1, 0:w_], scalar=self.sel[:, 1:2],
                                                            in1=tmp[:, 0, 0:w_], op0=ALU.mult, op1=ALU.add),
                    rd=[tmp_tk[0], tmp_tk[1], self.k_const], wr=() if pw else dst_tks, pw=dst_tks if pw else ())
        else:
            raise NotImplementedError

    def phase_gla(self, l, ctx_out):
        NCH = NTK // 64
        out_tiles = list(range(18)) if ctx_out else list(range(16))
        with self.scope() as sg:
            scanmask = self.sb("scanmask", [128, NTK], F32, sg)
            zlowT = self.sb("zlowT", [48, NTK], F32, sg)
            k_scan, k_zl = Tk("scanmask"), Tk("zlowT")
            self.op("dve", lambda e: e.memset(scanmask[:], 1.0), wr=[k_scan])
            self.op("dve", lambda e: e.memset(scanmask[:, 0:NTK:64], 0.0), rd=[k_scan], wr=[k_scan])

            def ev_zl(t, t0, n, pb, pk):
                self.op("act", lambda e: e.copy(out=zlowT[:, t0:t0 + n], in_=pb[0:48, 0:n]), rd=[pk], pw=[k_zl])
            self.inproj(l, None, [0, 1, 2, 3, 4], ev_zl, src=self.d_zl[l, :, :], ncols=48)
            self.stop_at("g_zl")

            for pr in range(2):
                with self.scope() as sp_:
                    qin = self.sb("qin", [128, 2, NTK], BF16, sp_)
                    kin = self.sb("kin", [128, 2, NTK], BF16, sp_)
                    SG = self.sb("SG", [128, NTK], BF16, sp_)
                    V2 = self.sb("gV2", [128, 36, 128], BF16, sp_)
                    dec = self.sb("dec", [128, 2, NCH], F32, sp_)
                    k_qin = [[Tk(f"qin{d}_{t}") for t in range(5)] for d in range(2)]
                    k_kin = [[Tk(f"kin{d}_{t}") for t in range(5)] for d in range(2)]
                    k_sg = [Tk(f"SG{t}") for t in range(5)]
                    k_v = [Tk(f"gV{t}") for t in range(5)]
                    k_dec = Tk("dec")
                    with self.scope() as sa:
                        Eb = self.sb("Eb", [128, 4, NTK], F32, sa)
                        k_E = [[Tk(f"E{j}_{t}") for t in range(5)] for j in range(4)]
                        for d in range(2):
                            r0 = 0 if d == 0 else 32
                            lap, cp = 2 * d, 2 * d + 1
                            for t in range(5):
                                t0, n = TT[t]
                                pb, pk = self.bank()
                                self.mm(pk, pb[:, 0:n], self.wa2[r0:r0 + 16, l, pr * 128:(pr + 1) * 128],
                                        zlowT[r0:r0 + 16, t0:t0 + n], True, True, rd=[self.k_const, k_zl])
                                self.op("act", lambda e, t0=t0, n=n, pb=pb: e.activation(
                                    out=Eb[:, lap, t0:t0 + n], in_=pb[:, 0:n], func=AF.Exp, scale=-1.0,
                                    bias=self.ngba[:, l, pr * 2 + d:pr * 2 + d + 1]),
                                    rd=[pk, self.k_const], wr=[k_E[lap][t]])
                                self.op("act", lambda e, t0=t0, n=n: e.activation(
                                    out=Eb[:, lap, t0:t0 + n], in_=Eb[:, lap, t0:t0 + n], func=AF.Ln, scale=1.0,
                                    bias=self.onesf[:, 0:1]),
                                    rd=[self.k_const], wr=[k_E[lap][t]])
                            for (a, b_) in ((0, NT), (NT, NTK)):
                                tl = [0, 1, 2, 3] if a == 0 else [4]
                                if d == 0:
                                    self.op("dve", lambda e, a=a, b_=b_: e.tensor_tensor_scan(
                                        out=Eb[:, cp, a:b_], data0=scanmask[:, a:b_], data1=Eb[:, lap, a:b_],
                                        initial=0.0, op0=ALU.mult, op1=ALU.add),
                                        rd=[k_scan] + [k_E[lap][t] for t in tl], wr=[k_E[cp][t] for t in tl])
                                else:
                                    self.op("dve", lambda e, a=a, b_=b_: e.tensor_tensor_scan(
                                        out=Eb[:, cp, b_ - 1:(a - 1 if a > 0 else None):-1], data0=scanmask[:, a:b_],
                                        data1=Eb[:, lap, b_ - 1:(a - 1 if a > 0 else None):-1],
                                        initial=0.0, op0=ALU.mult, op1=ALU.add),
                                        rd=[k_scan] + [k_E[lap][t] for t in tl], wr=[k_E[cp][t] for t in tl])
                            for t in range(5):
                                t0, n = TT[t]
                                self.op("act", lambda e, t0=t0, n=n: e.activation(
                                    out=Eb[:, lap, t0:t0 + n], in_=Eb[:, cp, t0:t0 + n], func=AF.Exp, scale=-1.0 / 16),
                                    rd=[k_E[cp][t]], wr=[k_E[lap][t]])
                                self.op("act", lambda e, t0=t0, n=n: e.activation(
                                    out=Eb[:, cp, t0:t0 + n], in_=Eb[:, cp, t0:t0 + n], func=AF.Exp, scale=1.0 / 16),
                                    rd=[], wr=[k_E[cp][t]])
                            off = 63 if d == 0 else 0
                            self.op("dve", lambda e, off=off: e.tensor_copy(out=dec[:, d, :], in_=Eb[:, lap, off:NTK:64]),
                                    rd=[k_E[lap][t] for t in range(5)], pw=[k_dec])
                        self.stop_at("g_decay")
                        def ev_q(t, t0, n, pb, pk):
                            for d in range(2):
                                self.op("dve", lambda e, d=d: e.scalar_tensor_tensor(
                                    out=qin[:, d, t0:t0 + n], in0=pb[:, 0:n], scalar=0.125, in1=Eb[:, 2 * d, t0:t0 + n],
                                    op0=ALU.mult, op1=ALU.mult), rd=[pk, k_E[2 * d][t]], wr=[k_qin[d][t]])

                        def ev_k(t, t0, n, pb, pk):
                            for d in range(2):
                                self.op("dve", lambda e, d=d: e.tensor_tensor(
                                    out=kin[:, d, t0:t0 + n], in0=pb[:, 0:n], in1=Eb[:, 2 * d + 1, t0:t0 + n],
                                    op=ALU.mult), rd=[pk, k_E[2 * d + 1][t]], wr=[k_kin[d][t]])

                        def ev_g(t, t0, n, pb, pk):
                            self.op("act", lambda e: e.activation(out=SG[:, t0:t0 + n], in_=pb[:, 0:n], func=AF.Silu),
                                    rd=[pk], wr=[k_sg[t]])

                        def ev_v(grp, pb, pk):
                            i0 = grp[0]
                            ng = len(grp)
                            src = pb[:, 0:ng * 128].rearrange("p (a b) -> p a b", a=ng)
                            c0, c1 = 2 * i0, 2 * i0 + 2 * ng
                            tk = [k_v[i0 // 4]]
                            self.op("act", lambda e: e.copy(out=V2[0:64, c0:c1:2, :], in_=src[0:64, :, :]), rd=[pk], pw=tk)
                            self.op("dve", lambda e: e.tensor_copy(out=V2[64:128, c0 + 1:c1:2, :], in_=src[64:128, :, :]),
                                    rd=[pk], pw=tk)
                            self.op("act", lambda e: e.copy(out=V2[64:128, c0:c1:2, :], in_=src[0:64, :, :]), rd=[pk], pw=tk)
                            self.op("dve", lambda e: e.tensor_copy(out=V2[0:64, c0 + 1:c1:2, :], in_=src[64:128, :, :]),
                                    rd=[pk], pw=tk)
                        tl5 = [0, 1, 2, 3, 4]
                        self.inproj(l, B_GQ + 4 * pr, tl5, ev_q)
                        self.inproj(l, B_GK + 4 * pr, tl5, ev_k)
                        self.inproj(l, B_GG + 4 * pr, tl5, ev_g)
                        self.vproj(l, B_GV + 4 * pr, list(range(18)), ev_v)
                    self.stop_at("g_proj")
                    with self.scope() as sb_:
                        kinT = self.sb("kinT", [128, 18, 2, 128], BF16, sb_)
                        Sbf = self.sb("Sbf", [128, 2, NCH, 64], BF16, sb_)
                        Sf = self.sb("Sf", [128, 4, 64], F32, sb_)
                        Tt = self.sb("Tt", [128, 4, 64], F32, sb_)
                        AT = self.sb("AT", [128, 3, 256], BF16, sb_)
                        osq = self.sb("osq", [128, 2, 512], BF16, sb_)
                        ors = self.sb("ors", [128, 2, 512], F32, sb_)
                        ot1 = self.sb("ot1", [128, 2, 512], F32, sb_)
                        rtmp = self.sb("rtmp", [128, 2, 64], F32, sb_)
                        k_kinT = [Tk(f"kinT{i}") for i in range(18)]
                        k_sbf = [[Tk(f"Sbf{d}_{n}") for n in range(NCH)] for d in range(2)]
                        k_S = [Tk(f"Sf{j}") for j in range(4)]
                        k_T = [Tk(f"Tt{j}") for j in range(4)]
                        k_AT = [Tk(f"AT{j}") for j in range(3)]
                        k_osq = [Tk("osq0"), Tk("osq1")]
                        k_ors = [Tk("ors0"), Tk("ors1")]
                        k_ot1 = [Tk("ot10"), Tk("ot11")]
                        k_rtmp = [Tk("rtmp0"), Tk("rtmp1")]
                        for i in range(18):
                            hb = i % 2
                            for d in range(2):
                                self.op("pe", lambda e, d=d: e.transpose(
                                    out=self.pbt[hb][:, d * 128:(d + 1) * 128],
                                    in_=kin[:, d, i * 128:(i + 1) * 128], identity=self.identb[:]),
                                    rd=[k_kin[d][i // 4], self.k_const], wr=[self.pbt_tk[hb]] if d == 0 else (),
                                    pw=[self.pbt_tk[hb]] if d else (), sig=(d == 1))
                            cpy = (lambda e: e.tensor_copy(out=kinT[:, i, :, :], in_=self.pbt[hb][:, 0:256]
                                                           .rearrange("p (a b) -> p a b", a=2)))
                            if i % 2 == 0:
                                self.op("dve", cpy, rd=[self.pbt_tk[hb]], wr=[k_kinT[i]])
                            else:
                                self.op("pool" if False else "dve", cpy, rd=[self.pbt_tk[hb]], wr=[k_kinT[i]])

                        self.stop_at("g_kinT")
                        xreg = {}
                        xcnt = [0]

                        def chain(d, chunks, S_init):
                            cur = S_init
                            for n in chunks:
                                i, cih = n // 2, n % 2
                                self.op("pool", lambda e, cur=cur, n=n: e.tensor_copy(out=Sbf[:, d, n, :], in_=Sf[:, cur, :]),
                                        rd=[k_S[cur]], wr=[k_sbf[d][n]])
                                r = 0
                                pb, xk = self.bank()
                                for h in range(2):
                                    self.op("pe", lambda e, h=h, pb=pb, r=r: e.matmul(
                                        pb[h * 64:(h + 1) * 64, r * 64:(r + 1) * 64],
                                        lhsT=kinT[cih * 64:(cih + 1) * 64, i, d, h * 64:(h + 1) * 64],
                                        rhs=V2[cih * 64:(cih + 1) * 64, n, h * 64:(h + 1) * 64], start=True, stop=True),
                                        rd=[k_kinT[i], k_v[i // 4]], wr=[xk] if h == 0 else (), pw=[xk] if h else (),
                                        sig=(h == 1))
                                tq = (cur + 1) % 2 + 2 * d
                                nxt = (cur + 1) % 2 + 2 * d
                                self.op("act", lambda e, pb=pb, r=r, tq=tq, n=n: e.activation(
                                    out=Tt[:, tq, :], in_=pb[:, r * 64:(r + 1) * 64], func=AF.Identity,
                                    scale=dec[:, d, n:n + 1]), rd=[xk, k_dec], wr=[k_T[tq]])
                                self.op("dve", lambda e, tq=tq, nxt=nxt, n=n, cur=cur: e.scalar_tensor_tensor(
                                    out=Sf[:, nxt, :], in0=Sf[:, cur, :], scalar=dec[:, d, n:n + 1], in1=Tt[:, tq, :],
                                    op0=ALU.mult, op1=ALU.add), rd=[k_T[tq], k_S[cur], k_dec], wr=[k_S[nxt]])
                                cur = nxt
                            return cur

                        self.op("dve", lambda e: e.memset(Sf[:, 0, :], 0.0), wr=[k_S[0]])
                        fin = chain(0, [32, 33, 34, 35] + list(range(32)), 0)
                        self.stop_at("g_chain0")
                        rcv, cev = self.exchange(f"gla{l}_{pr}", [(0, Sf[:, fin, :])], 64, F32, rd=[k_S[fin]])
                        if ctx_out:
                            self.op("dve", lambda e: e.memset(Sf[:, 2, :], 0.0), wr=[k_S[2]])
                            chain(1, [35, 34, 33, 32], 2)
                        self.recv_blend(rcv, cev, 0, 64, Sf[:, 2, :], [k_S[2]], rtmp, k_rtmp)
                        chain(1, list(range(31, -1, -1)), 2)

                        self.stop_at("g_chain1")
                        ai = 0
                        for gi, g0 in enumerate(range(0, len(out_tiles), 4)):
                            grp = out_tiles[g0:g0 + 4]
                            n = len(grp) * 128
                            t = grp[0] // 4
                            t0 = grp[0] * 128
                            po = [self.bank(0), self.bank(1)]
                            for j, i in enumerate(grp):
                                a = ai % 3
                                ai += 1
                                pa = [self.bank(2 + 2 * (ai % 2)), self.bank(3 + 2 * (ai % 2))]
                                for h in range(2):
                                    hs = slice(h * 64, (h + 1) * 64)
                                    pab, pak = pa[h]
                                    first = True
                                    for cih in range(2):
                                        tok = slice(i * 128 + cih * 64, i * 128 + cih * 64 + 64)
                                        for d in range(2):
                                            c0 = (cih * 2 + d) * 64
                                            self.op("pe", lambda e, hs=hs, d=d, c0=c0, tok=tok, pab=pab: e.matmul(
                                                pab[hs, c0:c0 + 64], lhsT=kin[hs, d, tok], rhs=qin[hs, d, tok],
                                                start=True, stop=True),
                                                rd=[k_kin[d][t], k_qin[d][t]], wr=[pak] if first else (),
                                                pw=() if first else [pak], sig=(cih == 1 and d == 1))
                                            first = False
                                    self.op("dve", lambda e, hs=hs, a=a, pab=pab: e.tensor_tensor(
                                        out=AT[hs, a, :], in0=pab[hs, 0:256], in1=self.glamask[hs, :], op=ALU.mult),
                                        rd=[pak, self.k_const], wr=[k_AT[a]] if h == 0 else (), pw=[k_AT[a]] if h else ())
                                for h in range(2):
                                    hs = slice(h * 64, (h + 1) * 64)
                                    pob, pok = po[h]
                                    for cih in range(2):
                                        nchunk = 2 * i + cih
                                        tok = slice(i * 128 + cih * 64, i * 128 + cih * 64 + 64)
                                        oap = pob[hs, j * 128 + cih * 64:j * 128 + cih * 64 + 64]
                                        first_g = (j == 0 and cih == 0)
                                        ops = []
                                        for d in range(2):
                                            c0 = (cih * 2 + d) * 64
                                            ops.append((V2[hs, nchunk, hs], AT[hs, a, c0:c0 + 64], [k_v[i // 4], k_AT[a]]))
                                        for d in range(2):
                                            ops.append((Sbf[hs, d, nchunk, :], qin[hs, d, tok], [k_sbf[d][nchunk], k_qin[d][t]]))
                                        for q, (lh, rh, rdl) in enumerate(ops):
                                            self.op("pe", lambda e, lh=lh, rh=rh, q=q, oap=oap: e.matmul(
                                                oap, lhsT=lh, rhs=rh, start=(q == 0), stop=(q == 3)),
                                                rd=rdl, wr=[pok] if (first_g and q == 0) else (),
                                                pw=() if (first_g and q == 0) else [pok], sig=(q == 3))
                            s2 = gi % 2
                            for h in range(2):
                                hs = slice(h * 64, (h + 1) * 64)
                                self.op("act", lambda e, hs=hs, h=h: e.activation(out=osq[hs, s2, 0:n], in_=po[h][0][hs, 0:n],
                                                                                  func=AF.Square),
                                        rd=[po[h][1]], wr=[k_osq[s2]] if h == 0 else (), pw=[k_osq[s2]] if h else ())
                            pm, pmk = self.bank(2 + 2 * (ai % 2))
                            self.mm(pmk, pm[:, 0:n], self.bdiag[:], osq[:, s2, 0:n], True, True, rd=[k_osq[s2], self.k_const])
                            self.op("act", lambda e: e.activation(out=ors[:, s2, 0:n], in_=pm[:, 0:n], func=AF.Sqrt,
                                                                  bias=self.epsT[:, 0:1], scale=1.0),
                                    rd=[pmk, self.k_const], wr=[k_ors[s2]])
                            self.op("dve", lambda e: e.reciprocal(out=ors[:, s2, 0:n], in_=ors[:, s2, 0:n]),
                                    rd=[k_ors[s2]], wr=[k_ors[s2]])
                            for h in range(2):
                                hs = slice(h * 64, (h + 1) * 64)
                                self.op("dve", lambda e, hs=hs, h=h: e.scalar_tensor_tensor(
                                    out=ot1[hs, s2, 0:n], in0=po[h][0][hs, 0:n], scalar=self.gnorm[hs, l, pr:pr + 1],
                                    in1=ors[hs, s2, 0:n], op0=ALU.mult, op1=ALU.mult),
                                    rd=[po[h][1], k_ors[s2], self.k_const], wr=[k_ot1[s2]] if h == 0 else (),
                                    pw=[k_ot1[s2]] if h else ())
                            self.op("pool", lambda e: e.tensor_tensor(
                                out=self.YT[:, pr, t0:t0 + n], in0=ot1[:, s2, 0:n], in1=SG[:, t0:t0 + n], op=ALU.mult),
                                rd=[k_ot1[s2], k_sg[t]], wr=[self.yT_tk[pr][t]])

    def wload_to(self, src, n, dst, dst_tks, pw=True):
        s_ = self.wi % NSLOT
        self.wi += 1
        self.dma(out=self.WF[:, s_, 0:n], in_=src, wr=[self.wf_tk[s_]])
        self.op("pool", lambda e: e.tensor_copy(out=dst, in_=self.WF[:, s_, 0:n]), rd=[self.wf_tk[s_]],
                pw=dst_tks if pw else (), wr=() if pw else dst_tks)

    def phase_na(self, l, ctx_out):
        with self.scope() as sn:
            QT = self.sb("nQT", [128, 6, NTK], BF16, sn)
            KT = self.sb("nKT", [128, 3, NTK], BF16, sn)
            VA = self.sb("nVA", [128, 3, 20, 256], BF16, sn)
            HB = self.sb("nHB", [128, 1536], BF16, sn)
            htmp = self.sb("nhtmp", [128, 2, 1536], BF16, sn)
            tab = self.sb("ntab", [128, 2, 2048], BF16, sn)
            PT = self.sb("nPT", [128, 3, 256], BF16, sn)
            R = self.sb("nR", [128, 2, 256], F32, sn)
            k_q = [[Tk(f"nq{j}_{t}") for t in range(5)] for j in range(3)]
            k_k = [[Tk(f"nk{j}_{t}") for t in range(5)] for j in range(3)]
            k_va = [[Tk(f"nva{j}_{g}") for g in range(6)] for j in range(3)]
            k_ones, k_hb = Tk("nones"), Tk("nHB")
            k_htmp = [Tk("nht0"), Tk("nht1")]
            k_tab = [Tk("ntab0"), Tk("ntab1")]
            k_pt = [Tk(f"nPT{i}") for i in range(3)]
            k_r = [Tk("nR0"), Tk("nR1")]
            k_qz = Tk("nqz")
            for j_ in range(3):
                self.op("dve", lambda e, j_=j_: e.memset(VA[:, j_, :, :], 1.0), pw=[k_ones])
                self.op("dve", lambda e, j_=j_: e.memset(QT[64:128, 2 * j_, :], 0.0), pw=[k_qz])
                self.op("dve", lambda e, j_=j_: e.memset(QT[0:64, 2 * j_ + 1, :], 0.0), pw=[k_qz])
            tl5 = [0, 1, 2, 3, 4]
            for j in range(3):
                def ev_q(t, t0, n, pb, pk, j=j):
                    self.op("act", lambda e: e.activation(out=QT[0:64, 2 * j, t0:t0 + n], in_=pb[0:64, 0:n], func=AF.Identity,
                                                          scale=0.125), rd=[pk], wr=[k_q[j][t]])
                    self.op("act", lambda e: e.activation(out=QT[64:128, 2 * j + 1, t0:t0 + n], in_=pb[64:128, 0:n],
                                                          func=AF.Identity, scale=0.125), rd=[pk], pw=[k_q[j][t]])

                def ev_k(t, t0, n, pb, pk, j=j):
                    self.op("dve", lambda e: e.tensor_copy(out=KT[:, j, t0:t0 + n], in_=pb[:, 0:n]), rd=[pk],
                            wr=[k_k[j][t]])

                def ev_v(grp, pb, pk, j=j):
                    i0, ng = grp[0], len(grp)
                    src = pb[:, 0:ng * 128].rearrange("p (a b) -> p a b", a=ng)
                    eng = "act" if (i0 // 4) % 2 == 0 else "dve"
                    for (c0, s0) in ((0, 0), (192, 64)):
                        if eng == "act":
                            self.op("act", lambda e, c0=c0, s0=s0: e.copy(out=VA[:, j, i0:i0 + ng, c0:c0 + 64],
                                                                          in_=src[:, :, s0:s0 + 64]),
                                    rd=[pk, k_ones], pw=[k_va[j][i0 // 4]])
                        else:
                            self.op("dve", lambda e, c0=c0, s0=s0: e.tensor_copy(out=VA[:, j, i0:i0 + ng, c0:c0 + 64],
                                                                                 in_=src[:, :, s0:s0 + 64]),
                                    rd=[pk, k_ones], pw=[k_va[j][i0 // 4]])
                import os
                skip = os.environ.get("NA_SKIP", "")
                if "q" not in skip:
                    self.inproj(l, B_NQ + j, tl5 if ctx_out else [0, 1, 2, 3], ev_q)
                if "k" not in skip:
                    self.inproj(l, B_NK + j, tl5, ev_k)
                if "v" not in skip:
                    self.vproj(l, B_NV + j, list(range(18)), ev_v)
            self.stop_at("n_proj")
            sends = []
            for j in range(3):
                sends.append((j * 256, KT[:, j, 1792:2048]))
                for ti in range(2):
                    sends.append((768 + j * 256 + ti * 128, VA[:, j, 14 + ti, 0:64]))
                    sends.append((768 + j * 256 + ti * 128 + 64, VA[:, j, 14 + ti, 192:256]))
            rcv, cev = self.exchange(f"na{l}", sends, 1536, BF16,
                                     rd=[k_k[j][3] for j in range(3)] + [k_va[j][3] for j in range(3)])
            halo_done = [False]

            def finish_halo():
                if halo_done[0]:
                    return
                halo_done[0] = True
                self.recv_blend(rcv, cev, 0, 1536, HB[:, :], [k_hb], htmp, k_htmp)
                for j in range(3):
                    src = HB[:, 768 + j * 256:768 + (j + 1) * 256].rearrange("p (a b) -> p a b", a=2)
                    self.op("pool", lambda e: e.tensor_copy(out=VA[:, j, 18:20, 0:64], in_=src[:, :, 0:64]),
                            rd=[k_hb], pw=[k_va[j][5]])
                    self.op("pool", lambda e: e.tensor_copy(out=VA[:, j, 18:20, 192:256], in_=src[:, :, 64:128]),
                            rd=[k_hb], pw=[k_va[j][5]])

            if self.upto == "n_exch":
                finish_halo()
                self.stop_at("n_exch")
            ri = 0
            pti = 0
            for h in range(6):
                if h == 1:
                    self.stop_at("n_h0")
                j, hh = h // 2, h % 2
                tr = h % 2
                for half in range(2):
                    self.wload_to(self.d_natab[l, h, :, half * 1024:(half + 1) * 1024], 1024,
                                  tab[:, tr, half * 1024:(half + 1) * 1024], [k_tab[tr]], pw=(half == 1))
                qts = list(range(8)) + ([8] if ctx_out else [])
                for qt in qts:
                    q0 = qt * 256
                    keys = []
                    if qt < 8:
                        if qt == 0:
                            kl = [(jj, 896 + (6 - 2 * jj) * 64) for jj in range(4)]
                        else:
                            kl = [(2 * qt - 2 + jj, (10 - 2 * jj) * 64) for jj in range(6)]
                        for (kt, col) in kl:
                            if kt < 16:
                                keys.append((KT[:, j, kt * 128:(kt + 1) * 128], [k_k[j][kt // 4]],
                                             kt, k_va[j][kt // 4], col))
                            else:
                                finish_halo()
                                jj = kt - 16
                                hc = j * 256 + (128 if jj == 0 else 0)
                                keys.append((HB[:, hc:hc + 128], [k_hb], 19 - jj, k_va[j][5],
                                             1536 + jj * 256))
                    for c_ in range(2):
                        keys.append((KT[:, j, 2048 + c_ * 128:2048 + (c_ + 1) * 128], [k_k[j][4]],
                                     16 + c_, k_va[j][4], None))
                    if qt == 8:
                        keys = keys[-2:]
                    po, pok = self.bank(lo=0, hi=2)
                    tq = qt // 2 if qt < 8 else 4
                    for ki, (kap, ktk, vt, vtk, col) in enumerate(keys):
                        ps, pk = self.bank(lo=2, hi=6)
                        self.mm(pk, ps[:, 0:256], kap, QT[:, h, q0:q0 + 256], True, col is None,
                                rd=ktk + [k_q[j][tq], k_qz])
                        if col is not None:
                            self.mm(pk, ps[:, 0:256], self.identb[:], tab[:, tr, col:col + 256], False, True,
                                    rd=[k_tab[tr], self.k_const])
                        p_ = pti % 3
                        pti += 1
                        self.op("act", lambda e, p_=p_, ps=ps: e.activation(out=PT[:, p_, :], in_=ps[:, 0:256], func=AF.Exp),
                                rd=[pk], wr=[k_pt[p_]])
                        self.mm(pok, po[:, 0:256], VA[:, j, vt, hh * 128:(hh + 1) * 128], PT[:, p_, :], ki == 0,
                                ki == len(keys) - 1, rd=[vtk, k_ones, k_pt[p_]])
                    oh = slice(0, 64) if hh == 0 else slice(64, 128)
                    dh_ = slice(64, 128) if hh == 0 else slice(0, 64)
                    r_ = ri % 2
                    ri += 1
                    self.op("dve", lambda e, r_=r_, po=po: e.reciprocal(out=R[oh, r_, :], in_=po[dh_, 0:256]),
                            rd=[pok], wr=[k_r[r_]])
                    self.op("dve", lambda e, r_=r_, po=po: e.tensor_tensor(out=self.YT[oh, 2 + j, q0:q0 + 256],
                                                                           in0=po[oh, 0:256], in1=R[oh, r_, :], op=ALU.mult),
                            rd=[pok, k_r[r_]], pw=[self.yT_tk[2 + j][tq]])
            finish_halo()

    def phase_swa(self, l, ctx_out):
        with self.scope() as ss:
            QT = self.sb("sQT", [128, 2, 3, NTK], BF16, ss)
            KT = self.sb("sKT", [128, NTK], BF16, ss)
            VA = self.sb("sVA", [128, 19, 192], BF16, ss)
            HB = self.sb("sHB", [128, 256], BF16, ss)
            htmp = self.sb("shtmp", [128, 2, 256], BF16, ss)
            cosT = self.sb("cosT", [128, NT], F32, ss)
            sinT = self.sb("sinT", [128, NT], F32, ss)
            rt = self.sb("srt", [128, 4, 512], F32, ss)
            PT = self.sb("sPT", [128, 3, 384], BF16, ss)
            R = self.sb("sR", [128, 2, 384], F32, ss)
            dtmp = self.sb("sdtmp", [128, 2, 384], F32, ss)
            esT = self.sb("sesT", [128, 2, 384], F32, ss)
            k_q = [[Tk(f"sq{j}_{t}") for t in range(5)] for j in range(3)]
            k_k = [Tk(f"sk{t}") for t in range(5)]
            k_va = [Tk(f"sva{g}") for g in range(6)]
            k_ones, k_hb, k_rope, k_es = Tk("sones"), Tk("sHB"), Tk("rope"), Tk("esT")
            k_htmp = [Tk("sht0"), Tk("sht1")]
            k_rt = [Tk(f"srt{i}") for i in range(4)]
            k_pt = [Tk(f"sPT{i}") for i in range(3)]
            k_r = [Tk("sR0"), Tk("sR1")]
            k_dt = [Tk("sdt0"), Tk("sdt1")]
            self.dma(out=cosT[:], in_=self.d_cosT[:, :], pw=[k_rope])
            self.dma(out=sinT[:], in_=self.d_sinT[:, :], pw=[k_rope])
            self.op("dve", lambda e: e.memset(VA[:, :, :], 1.0), wr=[k_ones])
            k_qz = Tk("sqz")
            self.op("dve", lambda e: e.memset(QT[64:128, 0, :, :], 0.0), pw=[k_qz])
            self.op("dve", lambda e: e.memset(QT[0:64, 1, :, :], 0.0), pw=[k_qz])
            self.op("dve", lambda e: e.memset(esT[:], 0.0), wr=[k_es])
            for g in range(2):
                for i_ in range(3):
                    self.op("dve", lambda e, g=g, i_=i_: e.tensor_scalar(
                        out=esT[:, g, i_ * 128:(i_ + 1) * 128], in0=esT[:, g, i_ * 128:(i_ + 1) * 128],
                        scalar1=self.esink[:, l, 3 * g + i_:3 * g + i_ + 1], scalar2=None, op0=ALU.add),
                        rd=[self.k_const, k_es], wr=[k_es])
            rti = [0]

            def roped(blk, blkp, dsts, tks, tiles):
                w, wk = self.wload(self.d_win[l, blk, :, :], 1024)
                wp, wpk = self.wload(self.d_win[l, blkp, :, :], 1024)
                for t in tiles:
                    t0, n = TT[t]
                    pb, pk = self.bank()
                    for kc in range(8):
                        self.mm(pk, pb[:, 0:n], w[:, kc * 128:(kc + 1) * 128], self.HT[:, kc, t0:t0 + n], kc == 0, kc == 7,
                                rd=[wk, self.hT_tk[kc][t]])
                    if t == 4:
                        for qi_, (dap, rows) in enumerate(dsts(t0, n)):
                            self.op("act", lambda e, dap=dap, rows=rows: e.copy(out=dap, in_=pb[rows, 0:n]), rd=[pk],
                                    wr=[tks[t]] if qi_ == 0 else (), pw=[tks[t]] if qi_ else ())
                        continue
                    pb2, pk2 = self.bank()
                    for kc in range(8):
                        self.mm(pk2, pb2[:, 0:n], wp[:, kc * 128:(kc + 1) * 128], self.HT[:, kc, t0:t0 + n], kc == 0,
                                kc == 7, rd=[wpk, self.hT_tk[kc][t]])
                    a, b_ = rti[0] % 4, (rti[0] + 1) % 4
                    rti[0] += 2
                    self.op("dve", lambda e: e.tensor_tensor(out=rt[:, a, 0:n], in0=pb[:, 0:n], in1=cosT[:, t0:t0 + n],
                                                             op=ALU.mult), rd=[pk, k_rope], wr=[k_rt[a]])
                    self.op("dve", lambda e: e.tensor_tensor(out=rt[:, b_, 0:n], in0=pb2[:, 0:n], in1=sinT[:, t0:t0 + n],
                                                             op=ALU.mult), rd=[pk2, k_rope], wr=[k_rt[b_]])
                    for qi_, (dap, rows) in enumerate(dsts(t0, n)):
                        self.op("pool", lambda e, dap=dap, rows=rows: e.tensor_tensor(
                            out=dap, in0=rt[rows, a, 0:n], in1=rt[rows, b_, 0:n], op=ALU.add),
                            rd=[k_rt[a], k_rt[b_]], wr=[tks[t]] if qi_ == 0 else (), pw=[tks[t]] if qi_ else ())
            tl5 = [0, 1, 2, 3, 4]
            for j in range(3):
                roped(B_SQ + j, B_SQP + j,
                      lambda t0, n, j=j: [(QT[0:64, 0, j, t0:t0 + n], slice(0, 64)),
                                          (QT[64:128, 1, j, t0:t0 + n], slice(64, 128))],
                      k_q[j], tl5 if ctx_out else [0, 1, 2, 3])
            roped(B_SK, B_SKP, lambda t0, n: [(KT[:, t0:t0 + n], slice(0, 128))], k_k, tl5)

            def ev_v(grp, pb, pk):
                i0, ng = grp[0], len(grp)
                src = pb[:, 0:ng * 128].rearrange("p (a b) -> p a b", a=ng)
                eng = "act" if (i0 // 4) % 2 == 0 else "dve"
                for (c0, s0) in ((0, 0), (128, 64)):
                    if eng == "act":
                        self.op("act", lambda e, c0=c0, s0=s0: e.copy(out=VA[:, i0:i0 + ng, c0:c0 + 64],
                                                                      in_=src[:, :, s0:s0 + 64]),
                                rd=[pk, k_ones], pw=[k_va[i0 // 4]])
                    else:
                        self.op("dve", lambda e, c0=c0, s0=s0: e.tensor_copy(out=VA[:, i0:i0 + ng, c0:c0 + 64],
                                                                             in_=src[:, :, s0:s0 + 64]),
                                rd=[pk, k_ones], pw=[k_va[i0 // 4]])
            self.vproj(l, B_SV, list(range(18)), ev_v)
            rcv, cev = self.exchange(f"swa{l}", [(0, KT[:, 1920:2048]), (128, VA[:, 15, 0:64]), (192, VA[:, 15, 128:192])],
                                     256, BF16, rd=[k_k[3], k_va[3]])
            halo_done = [False]

            def finish_halo():
                if halo_done[0]:
                    return
                halo_done[0] = True
                self.recv_blend(rcv, cev, 0, 256, HB[:, :], [k_hb], htmp, k_htmp)
                self.op("pool", lambda e: e.tensor_copy(out=VA[:, 18, 0:64], in_=HB[:, 128:192]), rd=[k_hb], pw=[k_va[5]])
                self.op("pool", lambda e: e.tensor_copy(out=VA[:, 18, 128:192], in_=HB[:, 192:256]), rd=[k_hb], pw=[k_va[5]])

            pti = 0
            ri = 0
            qbs = list(range(16)) + ([16, 17] if ctx_out else [])
            for qb in qbs:
                q0 = qb * 128
                tq = qb // 4
                for g in range(2):
                    gs = slice(g * 64, (g + 1) * 64)
                    keys = []
                    if qb < 16:
                        if qb > 0:
                            keys.append((KT[:, (qb - 1) * 128:qb * 128], [k_k[(qb - 1) // 4]], qb - 1, k_va[(qb - 1) // 4], 0))
                        keys.append((KT[:, qb * 128:(qb + 1) * 128], [k_k[qb // 4]], qb, k_va[qb // 4], None))
                        if qb < 15:
                            keys.append((KT[:, (qb + 1) * 128:(qb + 2) * 128], [k_k[(qb + 1) // 4]], qb + 1,
                                         k_va[(qb + 1) // 4], 1))
                        else:
                            finish_halo()
                            keys.append((HB[:, 0:128], [k_hb], 18, k_va[5], 2))
                    for c_ in range(2):
                        keys.append((KT[:, 2048 + c_ * 128:2048 + (c_ + 1) * 128], [k_k[4]], 16 + c_, k_va[4], None))
                    po, pok = self.bank(lo=0, hi=2)
                    po3 = po[:, 0:384].rearrange("p (a b) -> p a b", a=3)
                    for ki, (kap, ktk, vt, vtk, mk) in enumerate(keys):
                        ps, pk = self.bank(lo=2, hi=6)
                        ps3 = ps[:, 0:384].rearrange("p (a b) -> p a b", a=3)
                        self.mm(pk, ps3, kap, QT[:, g, :, q0:q0 + 128], True, mk is None,
                                rd=ktk + [k_q[j_][tq] for j_ in range(3)] + [k_qz])
                        if mk is not None:
                            self.mm(pk, ps3, self.identb[:],
                                    self.swamask[:, mk * 384:(mk + 1) * 384].rearrange("p (a b) -> p a b", a=3),
                                    False, True, rd=[self.k_const])
                        p_ = pti % 3
                        pti += 1
                        self.op("act", lambda e, p_=p_, ps=ps: e.activation(out=PT[:, p_, :], in_=ps[:, 0:384], func=AF.Exp,
                                                                            scale=0.125), rd=[pk], wr=[k_pt[p_]])
                        self.mm(pok, po[:, 0:384], VA[:, vt, g * 64:g * 64 + 128], PT[:, p_, :], ki == 0,
                                ki == len(keys) - 1, rd=[vtk, k_ones, k_pt[p_]])
                    oh = slice(0, 64) if g == 0 else slice(64, 128)
                    dh_ = slice(64, 128) if g == 0 else slice(0, 64)
                    r_ = ri % 2
                    ri += 1
                    self.op("dve", lambda e, r_=r_, po=po: e.tensor_tensor(out=dtmp[dh_, r_, :], in0=po[dh_, 0:384],
                                                                           in1=esT[dh_, g, :], op=ALU.add),
                            rd=[pok, k_es], wr=[k_dt[r_]])
                    self.op("dve", lambda e, r_=r_: e.reciprocal(out=R[oh, r_, :], in_=dtmp[dh_, r_, :]),
                            rd=[k_dt[r_]], wr=[k_r[r_]])
                    self.op("dve", lambda e, r_=r_, po3=po3: e.tensor_tensor(
                        out=self.YT[oh, 5:8, q0:q0 + 128], in0=po3[oh, :, :],
                        in1=R[oh, r_, :].rearrange("p (a b) -> p a b", a=3), op=ALU.mult),
                        rd=[pok, k_r[r_]], pw=[self.yT_tk[5 + j_][tq] for j_ in range(3)])
            finish_halo()

    def phase_outproj(self, l, xT, tiles):
        for ob in range(8):
            w, wk = self.wload(self.d_wout[l, ob, :, :], 1024)
            for t in tiles:
                t0, n = TT[t]
                wsel = 1 if t == 4 else 0
                pb, pk = self.bank()
                for kc in range(8):
                    self.mm(pk, pb[:, 0:n], w[:, kc * 128:(kc + 1) * 128], self.YT[:, kc, t0:t0 + n], kc == 0, kc == 7,
                            rd=[wk, self.yT_tk[kc][t]])
                self.op("dve", lambda e, pb=pb, t0=t0, n=n, wsel=wsel: e.scalar_tensor_tensor(
                    out=xT[:, ob, t0:t0 + n], in0=pb[:, 0:n], scalar=self.modT[:, 16 + ob, wsel:wsel + 1],
                    in1=xT[:, ob, t0:t0 + n], op0=ALU.mult, op1=ALU.add),
                    rd=[pk, self.k_mod], wr=[self.xT_tk[ob][t]])

    def phase_ffn(self, l, xT, with_ctx):
        if with_ctx:
            groups = [[(0, 512, 0, 0), (512, 512, 1, 0), (2048, 128, 4, 1)],
                      [(1024, 512, 2, 0), (1536, 512, 3, 0), (2176, 128, 4, 1)]]
        else:
            groups = [[(0, 512, 0, 0), (512, 512, 1, 0)], [(1024, 512, 2, 0), (1536, 512, 3, 0)]]
        GW = 1152
        self.fence()
        aT1 = self.YT[:].rearrange("p a b -> p (a b)").rearrange("p (a b) -> p a b", a=16)
        k_aT = [Tk(f"aT{f}") for f in range(NFB)]
        with self.scope() as sf:
            sg = self.sb("fsg", [128, 2, 512], F32, sf)
            aT2 = self.sb("aT2", [128, NFB - 16, GW], BF16, sf)
            k_sg = [Tk("fsg0"), Tk("fsg1")]

            def aT(fb, o, n):
                return aT1[:, fb, o:o + n] if fb < 16 else aT2[:, fb - 16, o:o + n]
            si = 0
            for grp in groups:
                offs = []
                o = 0
                for (t0, n, t, wsel) in grp:
                    offs.append(o)
                    o += n
                for fb in range(NFB):
                    wg, wgk = self.wload(self.d_wgu[l, fb, 0, :, :], 1024)
                    wu, wuk = self.wload(self.d_wgu[l, fb, 1, :, :], 1024)
                    for pi, (t0, n, t, wsel) in enumerate(grp):
                        pg, pgk = self.bank(lo=3, hi=6)
                        for kc in range(8):
                            self.mm(pgk, pg[:, 0:n], wg[:, kc * 128:(kc + 1) * 128], self.HT[:, kc, t0:t0 + n], kc == 0,
                                    kc == 7, rd=[wgk, self.hT_tk[kc][t]])
                        pu, puk = self.bank(lo=3, hi=6)
                        for kc in range(8):
                            self.mm(puk, pu[:, 0:n], wu[:, kc * 128:(kc + 1) * 128], self.HT[:, kc, t0:t0 + n], kc == 0,
                                    kc == 7, rd=[wuk, self.hT_tk[kc][t]])
                        q = si % 2
                        si += 1
                        self.op("act", lambda e, q=q, pg=pg, n=n: e.activation(out=sg[:, q, 0:n], in_=pg[:, 0:n], func=AF.Silu),
                                rd=[pgk], wr=[k_sg[q]])
                        oo = offs[pi]
                        self.op("dve", lambda e, q=q, pu=pu, n=n, oo=oo: e.tensor_tensor(
                            out=aT(fb, oo, n), in0=pu[:, 0:n], in1=sg[:, q, 0:n], op=ALU.mult),
                            rd=[puk, k_sg[q]], wr=[k_aT[fb]] if pi == 0 else (), pw=[k_aT[fb]] if pi else ())
                for ob in range(8):
                    pbs = [self.bank(pi) for pi in range(len(grp))]
                    for half in range(2):
                        wd, wdk = self.wload(self.d_wdn[l, ob, half, :, :], 1408)
                        for pi, (t0, n, t, wsel) in enumerate(grp):
                            pb, pk = pbs[pi]
                            oo = offs[pi]
                            for kc in range(11):
                                kk = half * 11 + kc
                                self.mm(pk, pb[:, 0:n], wd[:, kc * 128:(kc + 1) * 128], aT(kk, oo, n), kk == 0, kk == 21,
                                        rd=[wdk, k_aT[kk]], sig=(kc == 10))
                    for pi, (t0, n, t, wsel) in enumerate(grp):
                        pb, pk = pbs[pi]
                        part = (n != 512)
                        self.op("dve", lambda e, pb=pb, t0=t0, n=n, wsel=wsel: e.scalar_tensor_tensor(
                            out=xT[:, ob, t0:t0 + n], in0=pb[:, 0:n], scalar=self.modT[:, 40 + ob, wsel:wsel + 1],
                            in1=xT[:, ob, t0:t0 + n], op0=ALU.mult, op1=ALU.add),
                            rd=[pk, self.k_mod, self.xT_tk[ob][t]], wr=() if part else [self.xT_tk[ob][t]],
                            pw=[self.xT_tk[ob][t]] if part else ())
        for k in range(8):
            for t in range(5):
                self.yT_tk[k][t].fence = Tk.cur_fence

    def phase_final(self, xT):
        with self.scope() as st:
            sq = self.sb("fsq", [128, 2, 4, 512], BF16, st)
            rs = self.sb("frs", [128, 2, 512], F32, st)
            fin = self.sb("fin", [128, 4, 512], F32, st)
            yo = self.sb("yo", [128, 2, 512], F32, st)
            sq_tk = [Tk("fsq0"), Tk("fsq1")]
            rs_tk = [Tk("frs0"), Tk("frs1")]
            fin_tk = [Tk(f"fin{j}") for j in range(4)]
            yo_tk = [Tk("yo0"), Tk("yo1")]
            hi = 0
            yi = 0
            for t in range(4):
                t0, n = TT[t]
                s = t % 2
                pb, pk = self.bank()
                for half in range(2):
                    hs = hi % 2
                    hi += 1
                    for j in range(4):
                        k = half * 4 + j
                        self.op("act", lambda e, k=k, j=j, hs=hs: e.activation(out=sq[:, hs, j, 0:n], in_=xT[:, k, t0:t0 + n],
                                                                               func=AF.Square),
                                rd=[self.xT_tk[k][t]], wr=[sq_tk[hs]] if j == 0 else (), pw=[sq_tk[hs]] if j else ())
                    for j in range(4):
                        k = half * 4 + j
                        self.mm(pk, pb[:, 0:n], self.onesb[:], sq[:, hs, j, 0:n], k == 0, k == 7,
                                rd=[sq_tk[hs], self.k_const], sig=(j == 3))
                self.op("act", lambda e, pb=pb: e.activation(out=rs[:, s, 0:n], in_=pb[:, 0:n], func=AF.Sqrt,
                                                             bias=self.epsT[:, 0:1], scale=1.0 / D),
                        rd=[pk, self.k_const], wr=[rs_tk[s]])
                self.op("dve", lambda e: e.reciprocal(out=rs[:, s, 0:n], in_=rs[:, s, 0:n]), rd=[rs_tk[s]], wr=[rs_tk[s]])
                for half in range(2):
                    for j in range(4):
                        k = half * 4 + j
                        self.op("dve", lambda e, k=k, j=j: e.scalar_tensor_tensor(
                            out=fin[:, j, 0:n], in0=xT[:, k, t0:t0 + n], scalar=self.fnorm[:, k:k + 1],
                            in1=rs[:, s, 0:n], op0=ALU.mult, op1=ALU.mult),
                            rd=[self.xT_tk[k][t], rs_tk[s], self.k_const], wr=[fin_tk[j]])
                    for sub in range(4):
                        i = t * 4 + sub
                        pt, ptk = self.bank()
                        for j in range(4):
                            self.op("pe", lambda e, j=j, pt=pt, sub=sub: e.transpose(
                                out=pt[:, j * 128:(j + 1) * 128], in_=fin[:, j, sub * 128:(sub + 1) * 128],
                                identity=self.identf[:]),
                                rd=[fin_tk[j], self.k_const], wr=[ptk] if j == 0 else (), pw=[ptk] if j else (),
                                sig=(j == 3))
                        y_ = yi % 2
                        yi += 1
                        if yi % 2:
                            self.op("act", lambda e, pt=pt, y_=y_: e.copy(out=yo[:, y_, :], in_=pt[:, 0:512]),
                                    rd=[ptk], wr=[yo_tk[y_]])
                        else:
                            self.op("dve", lambda e, pt=pt, y_=y_: e.tensor_copy(out=yo[:, y_, :], in_=pt[:, 0:512]),
                                    rd=[ptk], wr=[yo_tk[y_]])
                        self.out_evs.append(self.dma(out=self.d_y[i * 128:(i + 1) * 128, half * 512:(half + 1) * 512],
                                                     in_=yo[:, y_, :], rd=[yo_tk[y_]]))

    def build(self):
        self.setup()
        nc = self.nc
        upto = self.upto
        allh = [self.hT_tk[k][t] for k in range(8) for t in range(5)]
        ally = [self.yT_tk[k][t] for k in range(8) for t in range(5)]
        k_xs = [Tk(f"xsave{k}") for k in range(8)]
        st = self.scope()
        st.__enter__()
        xT = self.sb("xT", [128, 8, NTK], F32, st)
        self.phase_load(xT)
        try:
            for l in range(DEPTH):
                ctx_out = (l == 0)
                tiles = [0, 1, 2, 3, 4] if ctx_out else [0, 1, 2, 3]
                self.phase_mod(l)
                self.phase_norm(xT, self.A1, 0, [0, 1, 2, 3, 4])
                if upto == "norm":
                    self.dump("hT", self.HT[:], [128, 8, NTK], BF16, allh)
                    raise _Stop()
                for k in range(8):
                    self.dma(out=self.d_xsave[:, k * NTK:(k + 1) * NTK], in_=xT[:, k, :],
                             rd=[self.xT_tk[k][t] for t in range(5)], wr=[k_xs[k]], q="act")
                st.__exit__(None, None, None)
                if upto == "save":
                    raise _Stop()
                if upto in ("na", "swa", "n_proj", "n_exch", "n_h0"):
                    if upto != "swa":
                        self.phase_na(0, True)
                        self.dump("yT", self.YT[:, 2:5], [128, 3, NTK], BF16, ally)
                    else:
                        self.phase_swa(0, True)
                        self.dump("yT", self.YT[:, 5:8], [128, 3, NTK], BF16, ally)
                    raise _Stop()
                self.phase_gla(l, ctx_out)
                if upto == "gla":
                    self.dump("yT", self.YT[:, 0:2], [128, 2, NTK], BF16, ally)
                    raise _Stop()
                self.phase_na(l, ctx_out)
                self.phase_swa(l, ctx_out)
                if upto == "mix":
                    self.dump("yT", self.YT[:], [128, 8, NTK], BF16, ally)
                    raise _Stop()
                st = self.scope()
                st.__enter__()
                xT = self.sb("xT", [128, 8, NTK], F32, st)
                for k in range(8):
                    for t in range(5):
                        tk = self.xT_tk[k][t]
                        tk.w, tk.r, tk.base, tk.fence = {}, {}, None, Tk.cur_fence
                    self.dma(out=xT[:, k, :], in_=self.d_xsave[:, k * NTK:(k + 1) * NTK], rd=[k_xs[k]],
                             wr=[self.xT_tk[k][t] for t in range(5)], q="act")
                self.phase_outproj(l, xT, tiles)
                self.phase_norm(xT, self.A2, 24, tiles)
                self.phase_ffn(l, xT, ctx_out)
                if upto == f"layer{l}":
                    self.dump("xT", xT[:], [128, 8, NTK], F32, [self.xT_tk[k][t] for k in range(8) for t in range(5)])
                    raise _Stop()
            self.phase_final(xT)
            st.__exit__(None, None, None)
        except _Stop:
            for nm in ("pe", "act", "dve", "pool"):
                E = self.engs[nm]
                if E.count:
                    self.engs["sp"].wait(Ev(E.sem, E.count, E))
        self.finish()
        return nc


def _run(prog, maps):
    names = set(prog.din.keys())
    in_maps = [{k: v for k, v in m.items() if k in names} for m in maps]
    return run_bass_kernel_spmd(prog.nc, in_maps, core_ids=list(range(8)))


_CACHE = {}


def kernel(**inputs):
    maps = prep_inputs(inputs)
    if "prog" not in _CACHE:
        p = Prog()
        p.build()
        _CACHE["prog"] = p
    p = _CACHE["prog"]
    res = _run(p, maps)
    out = np.empty((4, SEQ, D), np.float32)
    for c in range(8):
        b, par = c // 2, c % 2
        y = np.asarray(res.results[c]["y"])
        if par == 0:
            out[b, 0:NT] = y
        else:
            out[b, NT:SEQ] = y[::-1]
    return out
```

```python
import numpy as np
import ml_dtypes
from contextlib import ExitStack
import concourse.bass as bass
import concourse.mybir as mybir
from concourse.bass_utils import run_bass_kernel_spmd

F32 = mybir.dt.float32
BF16 = mybir.dt.bfloat16
ALU = mybir.AluOpType
AF = mybir.ActivationFunctionType

D = 1024
SEQ = 4096
NT = 2048
LC = 256
NTK = NT + LC
DFF = 2816
NFB = DFF // 128
DEPTH = 2
EPS = 1e-6
NEG = -30000.0
NSLOT = 3
WSL = 1408
TT = [(0, 512), (512, 512), (1024, 512), (1536, 512), (2048, 256)]
SAME_ENGINE_SYNC = False


class Ev:
    __slots__ = ("sem", "val", "eng")

    def __init__(self, sem, val, eng):
        self.sem, self.val, self.eng = sem, val, eng


class Tk:
    __slots__ = ("name", "w", "r", "base", "fence")
    cur_fence = ()

    def __init__(self, name):
        self.name, self.w, self.r, self.base = name, {}, {}, None
        self.fence = Tk.cur_fence


class Eng:
    def __init__(self, b, name, e, is_pe=False):
        self.b, self.name, self.e, self.is_pe = b, name, e, is_pe
        self.sem = b.newsem(name)
        self.count = 0
        self.waited = {}
        self.pending = []
        self.nwait = 0
        self.ninst = 0

    def wait(self, ev):
        if ev is None:
            return
        if ev.eng is self and (self.is_pe or not SAME_ENGINE_SYNC):
            return
        assert ev.val is not None, f"wait on unsignaled access (eng {ev.eng.name}) from {self.name}"
        key = id(ev.sem)
        if self.waited.get(key, 0) >= ev.val:
            return
        self.e.wait_ge(ev.sem, ev.val)
        self.nwait += 1
        self.waited[key] = ev.val

    def signal(self, ins):
        if self.count >= 30000:
            self.sem = self.b.newsem(self.name)
            self.count = 0
        self.count += 1
        ins.then_inc(self.sem, 1)
        ev = Ev(self.sem, self.count, self)
        for p in self.pending:
            p.sem, p.val = ev.sem, ev.val
        self.pending = []
        return ev

    def defer(self):
        ev = Ev(None, None, self)
        self.pending.append(ev)
        return ev


class Builder:
    def __init__(self):
        self.nc = bass.Bass("TRN2", target_bir_lowering=False)
        self.es = ExitStack()
        self.nsem = 0
        nc = self.nc
        self.engs = {
            "pe": Eng(self, "pe", nc.tensor, is_pe=True),
            "act": Eng(self, "act", nc.scalar),
            "dve": Eng(self, "dve", nc.vector),
            "pool": Eng(self, "pool", nc.gpsimd),
            "sp": Eng(self, "sp", nc.sync),
        }
        self.dma_sems = []
        for i in range(12):
            self.dma_sems.append([self.newsem(f"dma{i}"), 0, None])
        self.dma_rr = 0
        self.din = {}
        self.dout = {}

    def newsem(self, name):
        self.nsem += 1
        return self.es.enter_context(self.nc.semaphore(f"{name}_{self.nsem}"))

    def sb(self, name, shape, dt, stack=None):
        self.nsb = getattr(self, "nsb", 0) + 1
        return (stack or self.es).enter_context(self.nc.sbuf_tensor(f"s{self.nsb}_{name}", shape, dt))

    def inp(self, name, shape, dt=F32):
        t = self.nc.dram_tensor(name, list(shape), dt, kind="ExternalInput").ap()
        self.din[name] = t
        return t

    def outp(self, name, shape, dt=F32):
        t = self.nc.dram_tensor(name, list(shape), dt, kind="ExternalOutput").ap()
        self.dout[name] = t
        return t

    def fence(self):
        evs = []
        for E in self.engs.values():
            assert not E.pending, f"fence with unsignaled {E.name} work"
            if E.count:
                evs.append(Ev(E.sem, E.count, E))
        for slot in self.dma_sems:
            if slot[2] is not None:
                evs.append(slot[2])
        Tk.cur_fence = tuple(evs)

    def scope(self):
        b = self

        class _S(ExitStack):
            def __exit__(self, *a):
                r = super().__exit__(*a)
                if a[0] is None:
                    b.fence()
                return r
        return _S()

    def _deps(self, E, rd, wr, pw):
        for lst in (rd, wr, pw):
            for t in lst:
                for ev in t.fence:
                    E.wait(ev)
        for t in rd:
            for ev in t.w.values():
                E.wait(ev)
        for t in wr:
            for ev in t.w.values():
                E.wait(ev)
            for ev in t.r.values():
                E.wait(ev)
        for t in pw:
            E.wait(t.base)
            for ev in t.r.values():
                E.wait(ev)

    def _mark(self, key, ev, rd, wr, pw):
        for t in rd:
            t.r[key] = ev
        for t in wr:
            t.w = {key: ev}
            t.base = ev
            t.r = {}
        for t in pw:
            t.w[key] = ev

    def op(self, eng, fn, rd=(), wr=(), pw=(), sig=True):
        E = self.engs[eng]
        self._deps(E, rd, wr, pw)
        ins = fn(E.e)
        E.ninst += 1
        ev = E.signal(ins) if sig else E.defer()
        self._mark(E.name, ev, rd, wr, pw)
        return ev

    def dma(self, out, in_, rd=(), wr=(), pw=(), q="sp"):
        E = self.engs[q]
        self._deps(E, rd, wr, pw)
        k = self.dma_rr
        self.dma_rr = (self.dma_rr + 1) % len(self.dma_sems)
        slot = self.dma_sems[k]
        if slot[2] is not None:
            E.wait(slot[2])
        if slot[1] >= 30000:
            slot[0] = self.newsem(f"dma{k}")
            slot[1] = 0
        slot[1] += 16
        E.e.dma_start(out=out, in_=in_).then_inc(slot[0], 16)
        E.ninst += 1
        ev = Ev(slot[0], slot[1], None)
        slot[2] = ev
        self._mark(f"dma{k}", ev, rd, wr, pw)
        return ev


def _blk(W, cols):
    K = W.shape[0]
    Wc = W[:, np.asarray(cols)]
    return np.ascontiguousarray(Wc.reshape(K // 128, 128, len(cols)).transpose(1, 0, 2))


def _vec(v, nb):
    return np.ascontiguousarray(v.reshape(nb, 128).T)


C_GQ, C_GK, C_GV, C_GG, C_ZF, C_ZB = 0, 256, 512, 768, 1024, 1040
C_NQ, C_NK, C_NV = 1056, 1440, 1824
C_SQ, C_SK, C_SV = 2208, 2592, 2720
SWA_BLOCK_HEADS = [(0, 3), (1, 4), (2, 5)]


def _win_blocks(w_in_l):
    r = np.arange
    blocks = []
    for pr in range(2):
        for base in (C_GQ, C_GK, C_GG, C_GV):
            blocks.append(_blk(w_in_l, base + pr * 128 + r(128)))
    for base in (C_NQ, C_NK, C_NV):
        for j in range(3):
            blocks.append(_blk(w_in_l, base + j * 128 + r(128)))
    perm = np.concatenate([r(32, 64), r(0, 32)])
    for use_perm in (False, True):
        for (ha, hb) in SWA_BLOCK_HEADS:
            idx = perm if use_perm else r(64)
            blocks.append(_blk(w_in_l, np.concatenate([C_SQ + 64 * ha + idx, C_SQ + 64 * hb + idx])))
    blocks.append(_blk(w_in_l, C_SK + r(128)))
    blocks.append(_blk(w_in_l, np.concatenate([C_SK + perm, C_SK + 64 + perm])))
    blocks.append(_blk(w_in_l, C_SV + r(128)))
    return np.stack(blocks).reshape(len(blocks), 128, 1024)


B_GQ, B_GK, B_GG, B_GV = 0, 1, 2, 3
B_NQ, B_NK, B_NV = 8, 11, 14
B_SQ, B_SQP, B_SK, B_SKP, B_SV = 17, 20, 23, 24, 25
NWIN = 26


def _yrows():
    rows = list(range(640))
    for (ha, hb) in SWA_BLOCK_HEADS:
        rows += list(range(640 + 64 * ha, 640 + 64 * ha + 64)) + list(range(640 + 64 * hb, 640 + 64 * hb + 64))
    return np.asarray(rows)


def _na_table(rpb_h, par):
    out = np.full((128, 2048), NEG, np.float32)
    kro = np.arange(128) // 64
    kcl = np.arange(128) % 64

    def fill(col0, dloc, kc_l, c_l, interior):
        if par == 0:
            dg, kc, c = dloc, kc_l, c_l
        else:
            dg, kc, c = -dloc, 63 - kc_l, 63 - c_l
        ws = np.clip(c - 8, 0, 48)
        ok = (kc >= ws) & (kc < ws + 16)
        if interior:
            ok = ok & (dg >= -4) & (dg <= 3)
        ok = ok & (dg + 7 >= 0) & (dg + 7 <= 14)
        dr = np.clip(dg + 7, 0, 14)
        dc = np.clip(kc - c + 15, 0, 30)
        vals = rpb_h[dr, dc]
        blk = np.where(ok, vals, np.float32(NEG)).astype(np.float32)
        out[:, col0:col0 + blk.shape[1]] = blk

    c64 = np.arange(64)[None, :]
    for u in range(14):
        fill(u * 64, (6 - u) + kro[:, None] + 0 * c64, kcl[:, None] + 0 * c64, c64 + 0 * kcl[:, None], True)
    for u in range(10):
        fill(896 + u * 64, (6 - u) + kro[:, None] + 0 * c64, kcl[:, None] + 0 * c64, c64 + 0 * kcl[:, None], False)
    for jj in range(2):
        for qr in range(4):
            krl = 32 + 2 * jj + (1 - kro)
            dloc = (krl - (28 + qr))[:, None] + 0 * c64
            fill(1536 + jj * 256 + qr * 64, dloc, (63 - kcl)[:, None] + 0 * c64, c64 + 0 * kcl[:, None], True)
    return out


def _rope_tables(par):
    i = np.arange(NT)
    t = i if par == 0 else (SEQ - 1 - i)
    row = (t // 64).astype(np.float32)
    col = (t % 64).astype(np.float32)
    inv = (np.float32(10000.0) ** (-np.arange(16, dtype=np.float32) / np.float32(16))).astype(np.float32)
    ang = np.concatenate([row[:, None] * inv, col[:, None] * inv], axis=-1).astype(np.float32)
    cs, sn = np.cos(ang).astype(np.float32), np.sin(ang).astype(np.float32)
    p = np.arange(128) % 64
    f = p % 32
    cosT = np.ascontiguousarray(cs[:, f].T)
    sinT = np.ascontiguousarray(sn[:, f].T) * np.where(p < 32, -1.0, 1.0).astype(np.float32)[:, None]
    return cosT.astype(np.float32), sinT.astype(np.float32)


def _consts():
    bf = ml_dtypes.bfloat16
    c = {}
    c["identf"] = np.eye(128, dtype=np.float32)
    c["identb"] = np.eye(128, dtype=np.float32).astype(bf)
    c["onesb"] = np.ones((128, 128), np.float32).astype(bf)
    bd = np.zeros((128, 128), np.float32)
    bd[:64, :64] = 1.0 / 64
    bd[64:, 64:] = 1.0 / 64
    c["bdiag"] = bd.astype(bf)
    s = (np.arange(128) % 64)[:, None]
    cc = np.arange(64)[None, :]
    m = np.zeros((128, 2, 2, 64), np.float32)
    m[:, :, 0, :] = (s <= cc)[:, None, :]
    m[:, :, 1, :] = (s >= cc)[:, None, :]
    c["glamask"] = m.reshape(128, 256).astype(bf)
    ki = np.arange(128)[:, None]
    qi = np.arange(128)[None, :]
    big = np.float32(NEG * 8)
    kinds = [ki >= qi, ki <= qi, (ki + qi) >= 127]
    sm = np.stack([np.tile(np.where(k, np.float32(0), big), (1, 3)) for k in kinds])
    c["swamask"] = np.ascontiguousarray(sm.transpose(1, 0, 2).reshape(128, 3 * 384)).astype(bf)
    return c


def prep_inputs(inp):
    f32 = np.float32
    g = {k: np.asarray(v) for k, v in inp.items()}
    shared = {}
    wmod_all = np.stack([np.stack([_blk(g["w_mod"][l], fb * 128 + np.arange(128)).reshape(128, 1024)
                                   for fb in range(48)]) for l in range(DEPTH)])
    bmod_all = np.stack([_vec(g["b_mod"][l], 48) for l in range(DEPTH)])
    shared["nmix"] = np.stack([_vec(g["norm_mix"][l], 8) for l in range(DEPTH)])
    shared["nffn"] = np.stack([_vec(g["norm_ffn"][l], 8) for l in range(DEPTH)])
    shared["fnorm"] = _vec(g["final_norm"], 8)
    shared["win"] = np.stack([_win_blocks(g["w_in"][l]) for l in range(DEPTH)])
    yr = _yrows()
    shared["wout"] = np.stack([np.stack([_blk(g["w_out"][l][yr], ob * 128 + np.arange(128)).reshape(128, 1024)
                                         for ob in range(8)]) for l in range(DEPTH)])
    shared["wgu"] = np.stack([np.stack([np.stack([_blk(g[nm][l], fb * 128 + np.arange(128)).reshape(128, 1024)
                                                  for nm in ("w_gate", "w_up")]) for fb in range(NFB)])
                              for l in range(DEPTH)])
    wdn = []
    for l in range(DEPTH):
        obs = []
        for ob in range(8):
            b = _blk(g["w_down"][l], ob * 128 + np.arange(128))
            obs.append(np.stack([b[:, 0:11].reshape(128, 1408), b[:, 11:22].reshape(128, 1408)]))
        wdn.append(np.stack(obs))
    shared["wdn"] = np.stack(wdn)
    shared["gnorm"] = np.stack([_vec(g["gla_norm"][l], 2) for l in range(DEPTH)])
    shared["sinkb"] = np.stack([np.tile(g["swa_sink"][l][None, :], (128, 1)) for l in range(DEPTH)]).astype(f32)
    shared.update(_consts())
    perpar = []
    for par in range(2):
        d = {}
        zf = [np.arange(C_ZF, C_ZF + 16), np.arange(C_ZB, C_ZB + 16)]
        wa = [g["gla_wa2_f"], g["gla_wa2_b"]]
        ba = [g["gla_ba_f"], g["gla_ba_b"]]
        lf, lb = (0, 1) if par == 0 else (1, 0)
        zl, wa2, gba = [], [], []
        for l in range(DEPTH):
            z = np.zeros((128, 8, 48), f32)
            z[:, :, 0:16] = _blk(g["w_in"][l], zf[lf])
            z[:, :, 32:48] = _blk(g["w_in"][l], zf[lb])
            zl.append(z.reshape(128, 384))
            w = np.zeros((48, 256), f32)
            w[0:16] = wa[lf][l]
            w[32:48] = wa[lb][l]
            wa2.append(w)
            b = np.zeros((128, 2, 2), f32)
            b[:, :, 0] = _vec(ba[lf][l], 2)
            b[:, :, 1] = _vec(ba[lb][l], 2)
            gba.append(b.reshape(128, 4))
        d["zl"], d["wa2"], d["gba"] = np.stack(zl), np.stack(wa2), np.stack(gba)
        d["natab"] = np.stack([np.stack([_na_table(g["na_rpb"][l][h], par) for h in range(6)]) for l in range(DEPTH)])
        d["cosT"], d["sinT"] = _rope_tables(par)
        sel = np.zeros((128, 2), f32)
        sel[:, 1 - par] = 1.0
        d["sel"] = sel
        d["wmodh"] = np.ascontiguousarray(wmod_all[:, par * 24:(par + 1) * 24])
        d["bmodh"] = np.ascontiguousarray(bmod_all[:, :, par * 24:(par + 1) * 24])
        perpar.append(d)
    maps = []
    for c in range(8):
        b, par = c // 2, c % 2
        m = dict(shared)
        m.update(perpar[par])
        if par == 0:
            m["x"] = np.ascontiguousarray(g["x"][b, 0:NT])
            m["ctx"] = np.ascontiguousarray(g["ctx"][b])
        else:
            m["x"] = np.ascontiguousarray(g["x"][b, NT:SEQ][::-1])
            m["ctx"] = np.ascontiguousarray(g["ctx"][b][::-1])
        cc = np.zeros((128, 8, 2), f32)
        cc[:, :, 0] = _vec(g["c"][b], 8)
        cc[:, :, 1] = _vec(g["c_ctx"], 8)
        m["cc"] = cc.reshape(128, 16)
        maps.append(m)
    return maps


class _Stop(Exception):
    pass


class Prog(Builder):
    DSHAPES = {
        "x": ([NT, D], F32),
        "ctx": ([LC, D], F32),
        "cc": ([128, 16], F32),
        "sel": ([128, 2], F32),
        "wmodh": ([DEPTH, 24, 128, 1024], F32),
        "bmodh": ([DEPTH, 128, 24], F32),
        "nmix": ([DEPTH, 128, 8], F32),
        "nffn": ([DEPTH, 128, 8], F32),
        "fnorm": ([128, 8], F32),
        "win": ([DEPTH, NWIN, 128, 1024], F32),
        "zl": ([DEPTH, 128, 384], F32),
        "wa2": ([DEPTH, 48, 256], F32),
        "gba": ([DEPTH, 128, 4], F32),
        "gnorm": ([DEPTH, 128, 2], F32),
        "natab": ([DEPTH, 6, 128, 2048], F32),
        "sinkb": ([DEPTH, 128, 6], F32),
        "wout": ([DEPTH, 8, 128, 1024], F32),
        "wgu": ([DEPTH, NFB, 2, 128, 1024], F32),
        "wdn": ([DEPTH, 8, 2, 128, 1408], F32),
        "cosT": ([128, NT], F32),
        "sinT": ([128, NT], F32),
        "identf": ([128, 128], F32),
        "identb": ([128, 128], BF16),
        "onesb": ([128, 128], BF16),
        "bdiag": ([128, 128], BF16),
        "glamask": ([128, 256], BF16),
        "swamask": ([128, 1152], BF16),
    }

    def __getattr__(self, name):
        if name.startswith("d_") and name[2:] in Prog.DSHAPES:
            sh, dt = Prog.DSHAPES[name[2:]]
            t = self.inp(name[2:], sh, dt)
            setattr(self, name, t)
            return t
        raise AttributeError(name)
    def __init__(self, upto="all", dbg=()):
        super().__init__()
        self.upto = upto
        self.dbgnames = set(dbg)
        self.out_evs = []
        self.wi = 0
        self.wbi = 0
        self.pbi = 0

    def setup(self):
        nc = self.nc
        self.d_y = self.outp("y", [NT, D])
        self.d_xsave = nc.dram_tensor("xsave", [128, 8 * NTK], F32).ap()

        sb = self.sb
        self.HT = sb("HT", [128, 8, NTK], BF16)
        self.YT = sb("YT", [128, 8, NTK], BF16)
        self.WF = sb("WF", [128, NSLOT, WSL], F32)
        self.WB = sb("WB", [128, NSLOT, WSL], BF16)
        self.wf_tk = [Tk(f"wf{i}") for i in range(NSLOT)]
        self.wb_tk = [Tk(f"wb{i}") for i in range(NSLOT)]
        self.hT_tk = [[Tk(f"hT{k}_{t}") for t in range(5)] for k in range(8)]
        self.yT_tk = [[Tk(f"yT{k}_{t}") for t in range(5)] for k in range(8)]
        self.xT_tk = [[Tk(f"xT{k}_{t}") for t in range(5)] for k in range(8)]
        self.identf = sb("identf", [128, 128], F32)
        self.identb = sb("identb", [128, 128], BF16)
        self.onesb = sb("onesb", [128, 128], BF16)
        self.bdiag = sb("bdiag", [128, 128], BF16)
        self.glamask = sb("glamask", [128, 256], BF16)
        self.swamask = sb("swamask", [128, 1152], BF16)
        self.sel = sb("sel", [128, 2], F32)
        self.cc = sb("cc", [128, 16], F32)
        self.sT = sb("sT", [128, 16], F32)
        self.bmod = sb("bmodh", [128, DEPTH, 24], F32)
        self.modH = sb("modH", [128, 24, 2], F32)
        self.k_modh = Tk("modH")
        self.nmix = sb("nmix", [128, DEPTH, 8], F32)
        self.nffn = sb("nffn", [128, DEPTH, 8], F32)
        self.fnorm = sb("fnorm", [128, 8], F32)
        self.gba = sb("gba", [128, DEPTH, 4], F32)
        self.ngba = sb("ngba", [128, DEPTH, 4], F32)
        self.gnorm = sb("gnorm", [128, DEPTH, 2], F32)
        self.sinkb = sb("sinkb", [128, DEPTH, 6], F32)
        self.esink = sb("esink", [128, DEPTH, 6], F32)
        self.wa2 = sb("wa2", [48, DEPTH, 256], F32)
        self.modT = sb("modT", [128, 48, 2], F32)
        self.A1 = sb("A1", [128, 8, 2], F32)
        self.A2 = sb("A2", [128, 8, 2], F32)
        self.epsT = sb("epsT", [128, 1], F32)
        self.k_const = Tk("const")
        self.k_mod = Tk("mod")
        self.k_s = Tk("sT")
        self.pb = [self.es.enter_context(nc.psum_tensor(f"pb{i}", [128, 512], F32)) for i in range(6)]
        self.pb_tk = [Tk(f"pb{i}") for i in range(6)]
        self.pbt = [self.es.enter_context(nc.psum_tensor(f"pbt{i}", [128, 1024], BF16)) for i in range(2)]
        self.pbt_tk = [Tk("pbt0"), Tk("pbt1")]
        self.onesf = sb("onesf", [128, 1], F32)

        K = self.k_const
        ld = lambda dst, src: self.dma(out=dst, in_=src, pw=[K])
        ld(self.identf[:], self.d_identf[:, :])
        ld(self.identb[:], self.d_identb[:, :])
        ld(self.onesb[:], self.d_onesb[:, :])
        ld(self.bdiag[:], self.d_bdiag[:, :])
        ld(self.glamask[:], self.d_glamask[:, :])
        ld(self.swamask[:], self.d_swamask[:, :])
        ld(self.sel[:], self.d_sel[:, :])
        ld(self.cc[:], self.d_cc[:, :])
        ld(self.fnorm[:], self.d_fnorm[:, :])
        for l in range(DEPTH):
            ld(self.bmod[:, l, :], self.d_bmodh[l, :, :])
            ld(self.nmix[:, l, :], self.d_nmix[l, :, :])
            ld(self.nffn[:, l, :], self.d_nffn[l, :, :])
            ld(self.gba[:, l, :], self.d_gba[l, :, :])
            ld(self.gnorm[:, l, :], self.d_gnorm[l, :, :])
            ld(self.sinkb[:, l, :], self.d_sinkb[l, :, :])
            ld(self.wa2[:, l, :], self.d_wa2[l, :, :])
        self.op("act", lambda e: e.activation(out=self.sT[:], in_=self.cc[:], func=AF.Silu), rd=[K], wr=[self.k_s])
        self.op("act", lambda e: e.activation(out=self.esink[:], in_=self.sinkb[:], func=AF.Exp), rd=[K], pw=[K])
        self.op("dve", lambda e: e.tensor_scalar(out=self.ngba[:], in0=self.gba[:], scalar1=-1.0, scalar2=None,
                                                 op0=ALU.mult), rd=[K], pw=[K])
        self.op("dve", lambda e: e.memset(self.epsT[:], EPS), pw=[K])
        self.op("dve", lambda e: e.memset(self.onesf[:], 1.0), pw=[K])

    def stop_at(self, name):
        if self.upto == name:
            raise _Stop()

    def bank(self, i=None, lo=0, hi=6):
        if i is None:
            key = (lo, hi)
            self.pbrr = getattr(self, "pbrr", {})
            c = self.pbrr.get(key, 0)
            self.pbrr[key] = c + 1
            i = lo + c % (hi - lo)
        return self.pb[i], self.pb_tk[i]

    def wload(self, src, n, cast=True):
        s = self.wi % NSLOT
        self.wi += 1
        self.dma(out=self.WF[:, s, 0:n], in_=src, wr=[self.wf_tk[s]])
        if not cast:
            return self.WF[:, s, 0:n], self.wf_tk[s]
        b = self.wbi % NSLOT
        self.wbi += 1
        self.op("pool", lambda e: e.tensor_copy(out=self.WB[:, b, 0:n], in_=self.WF[:, s, 0:n]),
                rd=[self.wf_tk[s]], wr=[self.wb_tk[b]])
        return self.WB[:, b, 0:n], self.wb_tk[b]

    def mm(self, pk, out, lhsT, rhs, first, last, rd, sig=None, **kw):
        return self.op("pe", lambda e: e.matmul(out, lhsT=lhsT, rhs=rhs, start=first, stop=last, **kw),
                       rd=rd, wr=[pk] if first else (), pw=() if first else [pk],
                       sig=last if sig is None else sig)

    def dump(self, name, ap, shape, dt, rd):
        if name not in self.dbgnames:
            return
        o = self.outp("dbg_" + name, shape, dt)
        self.out_evs.append(self.dma(out=o, in_=ap, rd=rd))

    def finish(self):
        E = self.engs["sp"]
        for ev in self.out_evs:
            E.wait(ev)

    def phase_load(self, xT):
        with self.scope() as st:
            xin = self.sb("xin", [128, 2, D], F32, st)
            xin_tk = [Tk("xin0"), Tk("xin1")]
            for i in range(18):
                src = self.d_x[i * 128:(i + 1) * 128, :] if i < 16 else self.d_ctx[(i - 16) * 128:(i - 15) * 128, :]
                s = i % 2
                t = i // 4
                self.dma(out=xin[:, s, :], in_=src, wr=[xin_tk[s]])
                for half in range(2):
                    pb, pk = self.bank()
                    for j in range(4):
                        kk = half * 4 + j
                        self.op("pe", lambda e, j=j, kk=kk: e.transpose(out=pb[:, j * 128:(j + 1) * 128],
                                                                        in_=xin[:, s, kk * 128:(kk + 1) * 128],
                                                                        identity=self.identf[:]),
                                rd=[xin_tk[s], self.k_const], wr=[pk] if j == 0 else (), pw=[pk] if j else (),
                                sig=(j == 3))
                    dst = xT[:, half * 4:half * 4 + 4, i * 128:(i + 1) * 128]
                    srcp = pb[:, 0:512].rearrange("p (a b) -> p a b", a=4)
                    tks = [self.xT_tk[half * 4 + j][t] for j in range(4)]
                    if half == 0:
                        self.op("dve", lambda e: e.tensor_copy(out=dst, in_=srcp), rd=[pk], pw=tks)
                    else:
                        self.op("act", lambda e: e.copy(out=dst, in_=srcp), rd=[pk], pw=tks)

    def phase_mod(self, l):
        pb, pk = self.bank()
        for fb in range(24):
            w, wk = self.wload(self.d_wmodh[l, fb, :, :], 1024, cast=False)
            for kc in range(8):
                self.mm(pk, pb[:, fb * 2:fb * 2 + 2], w[:, kc * 128:(kc + 1) * 128], self.sT[:, kc * 2:kc * 2 + 2],
                        kc == 0, kc == 7, rd=[wk, self.k_s])
        for w_ in range(2):
            self.op("dve", lambda e, w_=w_: e.tensor_tensor(out=self.modH[:, :, w_], in0=pb[:, w_:48:2],
                                                             in1=self.bmod[:, l, :], op=ALU.add),
                    rd=[pk, self.k_const], wr=[self.k_modh] if w_ == 0 else (), pw=[self.k_modh] if w_ else ())
        rcv, cev = self.exchange(f"mod{l}", [(0, self.modH[:].rearrange("p a b -> p (a b)"))], 48, F32, rd=[self.k_modh])
        tk = Tk("modrcv")
        tk.w = {"cc": cev}
        mflat = self.modT[:].rearrange("p a b -> p (a b)")
        self.dma(out=mflat[:, 0:48], in_=rcv[0:128, :], rd=[tk], wr=[self.k_mod])
        self.dma(out=mflat[:, 48:96], in_=rcv[128:256, :], rd=[tk], pw=[self.k_mod])
        for w_ in range(2):
            self.op("dve", lambda e, w_=w_: e.scalar_tensor_tensor(out=self.A1[:, :, w_], in0=self.modT[:, 8:16, w_],
                                                                    scalar=1.0, in1=self.nmix[:, l, :],
                                                                    op0=ALU.add, op1=ALU.mult),
                    rd=[self.k_mod, self.k_const], pw=[self.k_mod])
            self.op("dve", lambda e, w_=w_: e.scalar_tensor_tensor(out=self.A2[:, :, w_], in0=self.modT[:, 32:40, w_],
                                                                    scalar=1.0, in1=self.nffn[:, l, :],
                                                                    op0=ALU.add, op1=ALU.mult),
                    rd=[self.k_mod, self.k_const], pw=[self.k_mod])

    def phase_norm(self, xT, A, boff, tiles):
        with self.scope() as st:
            sq = self.sb("sq", [128, 2, 4, 512], BF16, st)
            rs = self.sb("rs", [128, 2, 512], F32, st)
            tmp = self.sb("ntmp", [128, 3, 512], F32, st)
            sq_tk = [Tk("sq0"), Tk("sq1")]
            rs_tk = [Tk("rs0"), Tk("rs1")]
            tmp_tk = [Tk(f"ntmp{i}") for i in range(3)]
            ti = 0
            hi = 0
            for it, t in enumerate(tiles):
                t0, n = TT[t]
                wsel = 1 if t == 4 else 0
                s = it % 2
                pb, pk = self.bank()
                for half in range(2):
                    hs = hi % 2
                    hi += 1
                    for j in range(4):
                        k = half * 4 + j
                        self.op("act", lambda e, k=k, j=j: e.activation(out=sq[:, hs, j, 0:n], in_=xT[:, k, t0:t0 + n],
                                                                        func=AF.Square),
                                rd=[self.xT_tk[k][t]], wr=[sq_tk[hs]] if j == 0 else (), pw=[sq_tk[hs]] if j else ())
                    for j in range(4):
                        k = half * 4 + j
                        self.mm(pk, pb[:, 0:n], self.onesb[:], sq[:, hs, j, 0:n], k == 0, k == 7,
                                rd=[sq_tk[hs], self.k_const], sig=(j == 3))
                self.op("act", lambda e, pb=pb: e.activation(out=pb[:, 0:n], in_=pb[:, 0:n], func=AF.Sqrt,
                                                             bias=self.epsT[:, 0:1], scale=1.0 / D),
                        rd=[self.k_const], wr=[pk])
                self.op("dve", lambda e, pb=pb: e.reciprocal(out=pb[:, 0:n], in_=pb[:, 0:n]), wr=[pk])
                for k in range(8):
                    q = ti % 3
                    ti += 1
                    self.op("dve", lambda e, k=k, q=q, pb=pb: e.scalar_tensor_tensor(
                        out=tmp[:, q, 0:n], in0=xT[:, k, t0:t0 + n], scalar=A[:, k, wsel:wsel + 1],
                        in1=pb[:, 0:n], op0=ALU.mult, op1=ALU.mult),
                        rd=[self.xT_tk[k][t], pk, self.k_mod], wr=[tmp_tk[q]])
                    self.op("act", lambda e, k=k, q=q: e.activation(
                        out=self.HT[:, k, t0:t0 + n], in_=tmp[:, q, 0:n], func=AF.Identity,
                        bias=self.modT[:, boff + k, wsel:wsel + 1], scale=1.0),
                        rd=[tmp_tk[q], self.k_mod], wr=[self.hT_tk[k][t]])

    def inproj(self, l, blk, tiles, evac, src=None, ncols=128):
        w, wk = self.wload(self.d_win[l, blk, :, :] if src is None else src, 8 * ncols)
        for t in tiles:
            t0, n = TT[t]
            pb, pk = self.bank()
            for kc in range(8):
                self.mm(pk, pb[0:ncols, 0:n], w[:, kc * ncols:(kc + 1) * ncols], self.HT[:, kc, t0:t0 + n],
                        kc == 0, kc == 7, rd=[wk, self.hT_tk[kc][t]])
            evac(t, t0, n, pb, pk)

    def vproj(self, l, blk, tiles128, evac):
        w, wk = self.wload(self.d_win[l, blk, :, :], 1024)
        for g0 in range(0, len(tiles128), 4):
            grp = tiles128[g0:g0 + 4]
            pb, pk = self.bank()
            for j, i in enumerate(grp):
                for kc in range(8):
                    self.mm(pk, pb[:, j * 128:(j + 1) * 128], self.HT[:, kc, i * 128:(i + 1) * 128],
                            w[:, kc * 128:(kc + 1) * 128], kc == 0, kc == 7, rd=[wk, self.hT_tk[kc][i // 4]])
            evac(grp, pb, pk)

    def exchange(self, name, send_aps, width, dt, rd):
        nc = self.nc
        snd = nc.dram_tensor("snd_" + name, [128, width], dt)
        rcv = nc.dram_tensor("rcv_" + name, [256, width], dt)
        evs = []
        for (c0, ap) in send_aps:
            w_ = ap.shape[-1] if len(ap.shape) == 2 else int(np.prod(ap.shape[1:]))
            evs.append(self.dma(out=snd.ap()[:, c0:c0 + w_], in_=ap, rd=rd))
        E = self.engs["pool"]
        for ev in evs:
            E.wait(ev)
        sem = self.newsem("cc_" + name)
        E.e.collective_compute("AllGather", ALU.bypass, replica_groups=[[0, 1], [2, 3], [4, 5], [6, 7]],
                               ins=[snd.ap().opt()], outs=[rcv.ap().opt()]).then_inc(sem)
        E.ninst += 1
        return rcv.ap(), Ev(sem, 1, None)

    def recv_blend(self, rcv, ev, c0, w_, dst, dst_tks, tmp, tmp_tk, eng="dve", pw=False):
        tk = Tk("rcvev")
        tk.w = {"cc": ev}
        self.dma(out=tmp[:, 0, 0:w_], in_=rcv[0:128, c0:c0 + w_], rd=[tk], wr=[tmp_tk[0]])
        self.dma(out=tmp[:, 1, 0:w_], in_=rcv[128:256, c0:c0 + w_], rd=[tk], wr=[tmp_tk[1]])
        self.op(eng, lambda e: e.tensor_scalar(out=tmp[:, 0, 0:w_], in0=tmp[:, 0, 0:w_], scalar1=self.sel[:, 0:1],
                                               scalar2=None, op0=ALU.mult), rd=[self.k_const], wr=[tmp_tk[0]])
        if eng == "dve":
            self.op("dve", lambda e: e.scalar_tensor_tensor(out=dst, in0=tmp[:, 1, 0:w_], scalar=self.sel[:, 1:2],
                                                            in1=tmp[:, 0, 0:w_], op0=ALU.mult, op1=ALU.add),
                    rd=[tmp_tk[0], tmp_tk[1], self.k_const], wr=() if pw else dst_tks, pw=dst_tks if pw else ())
        else:
            raise NotImplementedError

    def phase_gla(self, l, ctx_out):
        NCH = NTK // 64
        out_tiles = list(range(18)) if ctx_out else list(range(16))
        with self.scope() as sg:
            scanmask = self.sb("scanmask", [128, NTK], F32, sg)
            zlowT = self.sb("zlowT", [48, NTK], F32, sg)
            k_scan, k_zl = Tk("scanmask"), Tk("zlowT")
            self.op("dve", lambda e: e.memset(scanmask[:], 1.0), wr=[k_scan])
            self.op("dve", lambda e: e.memset(scanmask[:, 0:NTK:64], 0.0), rd=[k_scan], wr=[k_scan])

            def ev_zl(t, t0, n, pb, pk):
                self.op("act", lambda e: e.copy(out=zlowT[:, t0:t0 + n], in_=pb[0:48, 0:n]), rd=[pk], pw=[k_zl])
            self.inproj(l, None, [0, 1, 2, 3, 4], ev_zl, src=self.d_zl[l, :, :], ncols=48)
            self.stop_at("g_zl")

            for pr in range(2):
                with self.scope() as sp_:
                    qin = self.sb("qin", [128, 2, NTK], BF16, sp_)
                    kin = self.sb("kin", [128, 2, NTK], BF16, sp_)
                    SG = self.sb("SG", [128, NTK], BF16, sp_)
                    V2 = self.sb("gV2", [128, 36, 128], BF16, sp_)
                    dec = self.sb("dec", [128, 2, NCH], F32, sp_)
                    k_qin = [[Tk(f"qin{d}_{t}") for t in range(5)] for d in range(2)]
                    k_kin = [[Tk(f"kin{d}_{t}") for t in range(5)] for d in range(2)]
                    k_sg = [Tk(f"SG{t}") for t in range(5)]
                    k_v = [Tk(f"gV{t}") for t in range(5)]
                    k_dec = Tk("dec")
                    with self.scope() as sa:
                        Eb = self.sb("Eb", [128, 4, NTK], F32, sa)
                        k_E = [[Tk(f"E{j}_{t}") for t in range(5)] for j in range(4)]
                        for d in range(2):
                            r0 = 0 if d == 0 else 32
                            lap, cp = 2 * d, 2 * d + 1
                            for t in range(5):
                                t0, n = TT[t]
                                pb, pk = self.bank()
                                self.mm(pk, pb[:, 0:n], self.wa2[r0:r0 + 16, l, pr * 128:(pr + 1) * 128],
                                        zlowT[r0:r0 + 16, t0:t0 + n], True, True, rd=[self.k_const, k_zl])
                                self.op("act", lambda e, t0=t0, n=n, pb=pb: e.activation(
                                    out=Eb[:, lap, t0:t0 + n], in_=pb[:, 0:n], func=AF.Exp, scale=-1.0,
                                    bias=self.ngba[:, l, pr * 2 + d:pr * 2 + d + 1]),
                                    rd=[pk, self.k_const], wr=[k_E[lap][t]])
                                self.op("act", lambda e, t0=t0, n=n: e.activation(
                                    out=Eb[:, lap, t0:t0 + n], in_=Eb[:, lap, t0:t0 + n], func=AF.Ln, scale=1.0,
                                    bias=self.onesf[:, 0:1]),
                                    rd=[self.k_const], wr=[k_E[lap][t]])
                            for (a, b_) in ((0, NT), (NT, NTK)):
                                tl = [0, 1, 2, 3] if a == 0 else [4]
                                if d == 0:
                                    self.op("dve", lambda e, a=a, b_=b_: e.tensor_tensor_scan(
                                        out=Eb[:, cp, a:b_], data0=scanmask[:, a:b_], data1=Eb[:, lap, a:b_],
                                        initial=0.0, op0=ALU.mult, op1=ALU.add),
                                        rd=[k_scan] + [k_E[lap][t] for t in tl], wr=[k_E[cp][t] for t in tl])
                                else:
                                    self.op("dve", lambda e, a=a, b_=b_: e.tensor_tensor_scan(
                                        out=Eb[:, cp, b_ - 1:(a - 1 if a > 0 else None):-1], data0=scanmask[:, a:b_],
                                        data1=Eb[:, lap, b_ - 1:(a - 1 if a > 0 else None):-1],
                                        initial=0.0, op0=ALU.mult, op1=ALU.add),
                                        rd=[k_scan] + [k_E[lap][t] for t in tl], wr=[k_E[cp][t] for t in tl])
                            for t in range(5):
                                t0, n = TT[t]
                                self.op("act", lambda e, t0=t0, n=n: e.activation(
                                    out=Eb[:, lap, t0:t0 + n], in_=Eb[:, cp, t0:t0 + n], func=AF.Exp, scale=-1.0 / 16),
                                    rd=[k_E[cp][t]], wr=[k_E[lap][t]])
                                self.op("act", lambda e, t0=t0, n=n: e.activation(
                                    out=Eb[:, cp, t0:t0 + n], in_=Eb[:, cp, t0:t0 + n], func=AF.Exp, scale=1.0 / 16),
                                    rd=[], wr=[k_E[cp][t]])
                            off = 63 if d == 0 else 0
                            self.op("dve", lambda e, off=off: e.tensor_copy(out=dec[:, d, :], in_=Eb[:, lap, off:NTK:64]),
                                    rd=[k_E[lap][t] for t in range(5)], pw=[k_dec])
                        self.stop_at("g_decay")
                        def ev_q(t, t0, n, pb, pk):
                            for d in range(2):
                                self.op("dve", lambda e, d=d: e.scalar_tensor_tensor(
                                    out=qin[:, d, t0:t0 + n], in0=pb[:, 0:n], scalar=0.125, in1=Eb[:, 2 * d, t0:t0 + n],
                                    op0=ALU.mult, op1=ALU.mult), rd=[pk, k_E[2 * d][t]], wr=[k_qin[d][t]])

                        def ev_k(t, t0, n, pb, pk):
                            for d in range(2):
                                self.op("dve", lambda e, d=d: e.tensor_tensor(
                                    out=kin[:, d, t0:t0 + n], in0=pb[:, 0:n], in1=Eb[:, 2 * d + 1, t0:t0 + n],
                                    op=ALU.mult), rd=[pk, k_E[2 * d + 1][t]], wr=[k_kin[d][t]])

                        def ev_g(t, t0, n, pb, pk):
                            self.op("act", lambda e: e.activation(out=SG[:, t0:t0 + n], in_=pb[:, 0:n], func=AF.Silu),
                                    rd=[pk], wr=[k_sg[t]])

                        def ev_v(grp, pb, pk):
                            i0 = grp[0]
                            ng = len(grp)
                            src = pb[:, 0:ng * 128].rearrange("p (a b) -> p a b", a=ng)
                            c0, c1 = 2 * i0, 2 * i0 + 2 * ng
                            tk = [k_v[i0 // 4]]
                            self.op("act", lambda e: e.copy(out=V2[0:64, c0:c1:2, :], in_=src[0:64, :, :]), rd=[pk], pw=tk)
                            self.op("dve", lambda e: e.tensor_copy(out=V2[64:128, c0 + 1:c1:2, :], in_=src[64:128, :, :]),
                                    rd=[pk], pw=tk)
                            self.op("act", lambda e: e.copy(out=V2[64:128, c0:c1:2, :], in_=src[0:64, :, :]), rd=[pk], pw=tk)
                            self.op("dve", lambda e: e.tensor_copy(out=V2[0:64, c0 + 1:c1:2, :], in_=src[64:128, :, :]),
                                    rd=[pk], pw=tk)
                        tl5 = [0, 1, 2, 3, 4]
                        self.inproj(l, B_GQ + 4 * pr, tl5, ev_q)
                        self.inproj(l, B_GK + 4 * pr, tl5, ev_k)
                        self.inproj(l, B_GG + 4 * pr, tl5, ev_g)
                        self.vproj(l, B_GV + 4 * pr, list(range(18)), ev_v)
                    self.stop_at("g_proj")
                    with self.scope() as sb_:
                        kinT = self.sb("kinT", [128, 18, 2, 128], BF16, sb_)
                        Sbf = self.sb("Sbf", [128, 2, NCH, 64], BF16, sb_)
                        Sf = self.sb("Sf", [128, 4, 64], F32, sb_)
                        Tt = self.sb("Tt", [128, 4, 64], F32, sb_)
                        AT = self.sb("AT", [128, 3, 256], BF16, sb_)
                        osq = self.sb("osq", [128, 2, 512], BF16, sb_)
                        ors = self.sb("ors", [128, 2, 512], F32, sb_)
                        ot1 = self.sb("ot1", [128, 2, 512], F32, sb_)
                        rtmp = self.sb("rtmp", [128, 2, 64], F32, sb_)
                        k_kinT = [Tk(f"kinT{i}") for i in range(18)]
                        k_sbf = [[Tk(f"Sbf{d}_{n}") for n in range(NCH)] for d in range(2)]
                        k_S = [Tk(f"Sf{j}") for j in range(4)]
                        k_T = [Tk(f"Tt{j}") for j in range(4)]
                        k_AT = [Tk(f"AT{j}") for j in range(3)]
                        k_osq = [Tk("osq0"), Tk("osq1")]
                        k_ors = [Tk("ors0"), Tk("ors1")]
                        k_ot1 = [Tk("ot10"), Tk("ot11")]
                        k_rtmp = [Tk("rtmp0"), Tk("rtmp1")]
                        for i in range(18):
                            hb = i % 2
                            for d in range(2):
                                self.op("pe", lambda e, d=d: e.transpose(
                                    out=self.pbt[hb][:, d * 128:(d + 1) * 128],
                                    in_=kin[:, d, i * 128:(i + 1) * 128], identity=self.identb[:]),
                                    rd=[k_kin[d][i // 4], self.k_const], wr=[self.pbt_tk[hb]] if d == 0 else (),
                                    pw=[self.pbt_tk[hb]] if d else (), sig=(d == 1))
                            cpy = (lambda e: e.tensor_copy(out=kinT[:, i, :, :], in_=self.pbt[hb][:, 0:256]
                                                           .rearrange("p (a b) -> p a b", a=2)))
                            if i % 2 == 0:
                                self.op("dve", cpy, rd=[self.pbt_tk[hb]], wr=[k_kinT[i]])
                            else:
                                self.op("pool" if False else "dve", cpy, rd=[self.pbt_tk[hb]], wr=[k_kinT[i]])

                        self.stop_at("g_kinT")
                        xreg = {}
                        xcnt = [0]

                        def chain(d, chunks, S_init):
                            cur = S_init
                            for n in chunks:
                                i, cih = n // 2, n % 2
                                self.op("pool", lambda e, cur=cur, n=n: e.tensor_copy(out=Sbf[:, d, n, :], in_=Sf[:, cur, :]),
                                        rd=[k_S[cur]], wr=[k_sbf[d][n]])
                                r = 0
                                pb, xk = self.bank()
                                for h in range(2):
                                    self.op("pe", lambda e, h=h, pb=pb, r=r: e.matmul(
                                        pb[h * 64:(h + 1) * 64, r * 64:(r + 1) * 64],
                                        lhsT=kinT[cih * 64:(cih + 1) * 64, i, d, h * 64:(h + 1) * 64],
                                        rhs=V2[cih * 64:(cih + 1) * 64, n, h * 64:(h + 1) * 64], start=True, stop=True),
                                        rd=[k_kinT[i], k_v[i // 4]], wr=[xk] if h == 0 else (), pw=[xk] if h else (),
                                        sig=(h == 1))
                                tq = (cur + 1) % 2 + 2 * d
                                nxt = (cur + 1) % 2 + 2 * d
                                self.op("act", lambda e, pb=pb, r=r, tq=tq, n=n: e.activation(
                                    out=Tt[:, tq, :], in_=pb[:, r * 64:(r + 1) * 64], func=AF.Identity,
                                    scale=dec[:, d, n:n + 1]), rd=[xk, k_dec], wr=[k_T[tq]])
                                self.op("dve", lambda e, tq=tq, nxt=nxt, n=n, cur=cur: e.scalar_tensor_tensor(
                                    out=Sf[:, nxt, :], in0=Sf[:, cur, :], scalar=dec[:, d, n:n + 1], in1=Tt[:, tq, :],
                                    op0=ALU.mult, op1=ALU.add), rd=[k_T[tq], k_S[cur], k_dec], wr=[k_S[nxt]])
                                cur = nxt
                            return cur

                        self.op("dve", lambda e: e.memset(Sf[:, 0, :], 0.0), wr=[k_S[0]])
                        fin = chain(0, [32, 33, 34, 35] + list(range(32)), 0)
                        self.stop_at("g_chain0")
                        rcv, cev = self.exchange(f"gla{l}_{pr}", [(0, Sf[:, fin, :])], 64, F32, rd=[k_S[fin]])
                        if ctx_out:
                            self.op("dve", lambda e: e.memset(Sf[:, 2, :], 0.0), wr=[k_S[2]])
                            chain(1, [35, 34, 33, 32], 2)
                        self.recv_blend(rcv, cev, 0, 64, Sf[:, 2, :], [k_S[2]], rtmp, k_rtmp)
                        chain(1, list(range(31, -1, -1)), 2)

                        self.stop_at("g_chain1")
                        ai = 0
                        for gi, g0 in enumerate(range(0, len(out_tiles), 4)):
                            grp = out_tiles[g0:g0 + 4]
                            n = len(grp) * 128
                            t = grp[0] // 4
                            t0 = grp[0] * 128
                            po = [self.bank(0), self.bank(1)]
                            for j, i in enumerate(grp):
                                a = ai % 3
                                ai += 1
                                pa = [self.bank(2 + 2 * (ai % 2)), self.bank(3 + 2 * (ai % 2))]
                                for h in range(2):
                                    hs = slice(h * 64, (h + 1) * 64)
                                    pab, pak = pa[h]
                                    first = True
                                    for cih in range(2):
                                        tok = slice(i * 128 + cih * 64, i * 128 + cih * 64 + 64)
                                        for d in range(2):
                                            c0 = (cih * 2 + d) * 64
                                            self.op("pe", lambda e, hs=hs, d=d, c0=c0, tok=tok, pab=pab: e.matmul(
                                                pab[hs, c0:c0 + 64], lhsT=kin[hs, d, tok], rhs=qin[hs, d, tok],
                                                start=True, stop=True),
                                                rd=[k_kin[d][t], k_qin[d][t]], wr=[pak] if first else (),
                                                pw=() if first else [pak], sig=(cih == 1 and d == 1))
                                            first = False
                                    self.op("dve", lambda e, hs=hs, a=a, pab=pab: e.tensor_tensor(
                                        out=AT[hs, a, :], in0=pab[hs, 0:256], in1=self.glamask[hs, :], op=ALU.mult),
                                        rd=[pak, self.k_const], wr=[k_AT[a]] if h == 0 else (), pw=[k_AT[a]] if h else ())
                                for h in range(2):
                                    hs = slice(h * 64, (h + 1) * 64)
                                    pob, pok = po[h]
                                    for cih in range(2):
                                        nchunk = 2 * i + cih
                                        tok = slice(i * 128 + cih * 64, i * 128 + cih * 64 + 64)
                                        oap = pob[hs, j * 128 + cih * 64:j * 128 + cih * 64 + 64]
                                        first_g = (j == 0 and cih == 0)
                                        ops = []
                                        for d in range(2):
                                            c0 = (cih * 2 + d) * 64
                                            ops.append((V2[hs, nchunk, hs], AT[hs, a, c0:c0 + 64], [k_v[i // 4], k_AT[a]]))
                                        for d in range(2):
                                            ops.append((Sbf[hs, d, nchunk, :], qin[hs, d, tok], [k_sbf[d][nchunk], k_qin[d][t]]))
                                        for q, (lh, rh, rdl) in enumerate(ops):
                                            self.op("pe", lambda e, lh=lh, rh=rh, q=q, oap=oap: e.matmul(
                                                oap, lhsT=lh, rhs=rh, start=(q == 0), stop=(q == 3)),
                                                rd=rdl, wr=[pok] if (first_g and q == 0) else (),
                                                pw=() if (first_g and q == 0) else [pok], sig=(q == 3))
                            s2 = gi % 2
                            for h in range(2):
                                hs = slice(h * 64, (h + 1) * 64)
                                self.op("act", lambda e, hs=hs, h=h: e.activation(out=osq[hs, s2, 0:n], in_=po[h][0][hs, 0:n],
                                                                                  func=AF.Square),
                                        rd=[po[h][1]], wr=[k_osq[s2]] if h == 0 else (), pw=[k_osq[s2]] if h else ())
                            pm, pmk = self.bank(2 + 2 * (ai % 2))
                            self.mm(pmk, pm[:, 0:n], self.bdiag[:], osq[:, s2, 0:n], True, True, rd=[k_osq[s2], self.k_const])
                            self.op("act", lambda e: e.activation(out=ors[:, s2, 0:n], in_=pm[:, 0:n], func=AF.Sqrt,
                                                                  bias=self.epsT[:, 0:1], scale=1.0),
                                    rd=[pmk, self.k_const], wr=[k_ors[s2]])
                            self.op("dve", lambda e: e.reciprocal(out=ors[:, s2, 0:n], in_=ors[:, s2, 0:n]),
                                    rd=[k_ors[s2]], wr=[k_ors[s2]])
                            for h in range(2):
                                hs = slice(h * 64, (h + 1) * 64)
                                self.op("dve", lambda e, hs=hs, h=h: e.scalar_tensor_tensor(
                                    out=ot1[hs, s2, 0:n], in0=po[h][0][hs, 0:n], scalar=self.gnorm[hs, l, pr:pr + 1],
                                    in1=ors[hs, s2, 0:n], op0=ALU.mult, op1=ALU.mult),
                                    rd=[po[h][1], k_ors[s2], self.k_const], wr=[k_ot1[s2]] if h == 0 else (),
                                    pw=[k_ot1[s2]] if h else ())
                            self.op("pool", lambda e: e.tensor_tensor(
                                out=self.YT[:, pr, t0:t0 + n], in0=ot1[:, s2, 0:n], in1=SG[:, t0:t0 + n], op=ALU.mult),
                                rd=[k_ot1[s2], k_sg[t]], wr=[self.yT_tk[pr][t]])

    def wload_to(self, src, n, dst, dst_tks, pw=True):
        s_ = self.wi % NSLOT
        self.wi += 1
        self.dma(out=self.WF[:, s_, 0:n], in_=src, wr=[self.wf_tk[s_]])
        self.op("pool", lambda e: e.tensor_copy(out=dst, in_=self.WF[:, s_, 0:n]), rd=[self.wf_tk[s_]],
                pw=dst_tks if pw else (), wr=() if pw else dst_tks)

    def phase_na(self, l, ctx_out):
        with self.scope() as sn:
            QT = self.sb("nQT", [128, 6, NTK], BF16, sn)
            KT = self.sb("nKT", [128, 3, NTK], BF16, sn)
            VA = self.sb("nVA", [128, 3, 20, 256], BF16, sn)
            HB = self.sb("nHB", [128, 1536], BF16, sn)
            htmp = self.sb("nhtmp", [128, 2, 1536], BF16, sn)
            tab = self.sb("ntab", [128, 2, 2048], BF16, sn)
            PT = self.sb("nPT", [128, 3, 256], BF16, sn)
            R = self.sb("nR", [128, 2, 256], F32, sn)
            k_q = [[Tk(f"nq{j}_{t}") for t in range(5)] for j in range(3)]
            k_k = [[Tk(f"nk{j}_{t}") for t in range(5)] for j in range(3)]
            k_va = [[Tk(f"nva{j}_{g}") for g in range(6)] for j in range(3)]
            k_ones, k_hb = Tk("nones"), Tk("nHB")
            k_htmp = [Tk("nht0"), Tk("nht1")]
            k_tab = [Tk("ntab0"), Tk("ntab1")]
            k_pt = [Tk(f"nPT{i}") for i in range(3)]
            k_r = [Tk("nR0"), Tk("nR1")]
            k_qz = Tk("nqz")
            for j_ in range(3):
                self.op("dve", lambda e, j_=j_: e.memset(VA[:, j_, :, :], 1.0), pw=[k_ones])
                self.op("dve", lambda e, j_=j_: e.memset(QT[64:128, 2 * j_, :], 0.0), pw=[k_qz])
                self.op("dve", lambda e, j_=j_: e.memset(QT[0:64, 2 * j_ + 1, :], 0.0), pw=[k_qz])
            tl5 = [0, 1, 2, 3, 4]
            for j in range(3):
                def ev_q(t, t0, n, pb, pk, j=j):
                    self.op("act", lambda e: e.activation(out=QT[0:64, 2 * j, t0:t0 + n], in_=pb[0:64, 0:n], func=AF.Identity,
                                                          scale=0.125), rd=[pk], wr=[k_q[j][t]])
                    self.op("act", lambda e: e.activation(out=QT[64:128, 2 * j + 1, t0:t0 + n], in_=pb[64:128, 0:n],
                                                          func=AF.Identity, scale=0.125), rd=[pk], pw=[k_q[j][t]])

                def ev_k(t, t0, n, pb, pk, j=j):
                    self.op("dve", lambda e: e.tensor_copy(out=KT[:, j, t0:t0 + n], in_=pb[:, 0:n]), rd=[pk],
                            wr=[k_k[j][t]])

                def ev_v(grp, pb, pk, j=j):
                    i0, ng = grp[0], len(grp)
                    src = pb[:, 0:ng * 128].rearrange("p (a b) -> p a b", a=ng)
                    eng = "act" if (i0 // 4) % 2 == 0 else "dve"
                    for (c0, s0) in ((0, 0), (192, 64)):
                        if eng == "act":
                            self.op("act", lambda e, c0=c0, s0=s0: e.copy(out=VA[:, j, i0:i0 + ng, c0:c0 + 64],
                                                                          in_=src[:, :, s0:s0 + 64]),
                                    rd=[pk, k_ones], pw=[k_va[j][i0 // 4]])
                        else:
                            self.op("dve", lambda e, c0=c0, s0=s0: e.tensor_copy(out=VA[:, j, i0:i0 + ng, c0:c0 + 64],
                                                                                 in_=src[:, :, s0:s0 + 64]),
                                    rd=[pk, k_ones], pw=[k_va[j][i0 // 4]])
                import os
                skip = os.environ.get("NA_SKIP", "")
                if "q" not in skip:
                    self.inproj(l, B_NQ + j, tl5 if ctx_out else [0, 1, 2, 3], ev_q)
                if "k" not in skip:
                    self.inproj(l, B_NK + j, tl5, ev_k)
                if "v" not in skip:
                    self.vproj(l, B_NV + j, list(range(18)), ev_v)
            self.stop_at("n_proj")
            sends = []
            for j in range(3):
                sends.append((j * 256, KT[:, j, 1792:2048]))
                for ti in range(2):
                    sends.append((768 + j * 256 + ti * 128, VA[:, j, 14 + ti, 0:64]))
                    sends.append((768 + j * 256 + ti * 128 + 64, VA[:, j, 14 + ti, 192:256]))
            rcv, cev = self.exchange(f"na{l}", sends, 1536, BF16,
                                     rd=[k_k[j][3] for j in range(3)] + [k_va[j][3] for j in range(3)])
            halo_done = [False]

            def finish_halo():
                if halo_done[0]:
                    return
                halo_done[0] = True
                self.recv_blend(rcv, cev, 0, 1536, HB[:, :], [k_hb], htmp, k_htmp)
                for j in range(3):
                    src = HB[:, 768 + j * 256:768 + (j + 1) * 256].rearrange("p (a b) -> p a b", a=2)
                    self.op("pool", lambda e: e.tensor_copy(out=VA[:, j, 18:20, 0:64], in_=src[:, :, 0:64]),
                            rd=[k_hb], pw=[k_va[j][5]])
                    self.op("pool", lambda e: e.tensor_copy(out=VA[:, j, 18:20, 192:256], in_=src[:, :, 64:128]),
                            rd=[k_hb], pw=[k_va[j][5]])

            if self.upto == "n_exch":
                finish_halo()
                self.stop_at("n_exch")
            ri = 0
            pti = 0
            for h in range(6):
                if h == 1:
                    self.stop_at("n_h0")
                j, hh = h // 2, h % 2
                tr = h % 2
                for half in range(2):
                    self.wload_to(self.d_natab[l, h, :, half * 1024:(half + 1) * 1024], 1024,
                                  tab[:, tr, half * 1024:(half + 1) * 1024], [k_tab[tr]], pw=(half == 1))
                qts = list(range(8)) + ([8] if ctx_out else [])
                for qt in qts:
                    q0 = qt * 256
                    keys = []
                    if qt < 8:
                        if qt == 0:
                            kl = [(jj, 896 + (6 - 2 * jj) * 64) for jj in range(4)]
                        else:
                            kl = [(2 * qt - 2 + jj, (10 - 2 * jj) * 64) for jj in range(6)]
                        for (kt, col) in kl:
                            if kt < 16:
                                keys.append((KT[:, j, kt * 128:(kt + 1) * 128], [k_k[j][kt // 4]],
                                             kt, k_va[j][kt // 4], col))
                            else:
                                finish_halo()
                                jj = kt - 16
                                hc = j * 256 + (128 if jj == 0 else 0)
                                keys.append((HB[:, hc:hc + 128], [k_hb], 19 - jj, k_va[j][5],
                                             1536 + jj * 256))
                    for c_ in range(2):
                        keys.append((KT[:, j, 2048 + c_ * 128:2048 + (c_ + 1) * 128], [k_k[j][4]],
                                     16 + c_, k_va[j][4], None))
                    if qt == 8:
                        keys = keys[-2:]
                    po, pok = self.bank(lo=0, hi=2)
                    tq = qt // 2 if qt < 8 else 4
                    for ki, (kap, ktk, vt, vtk, col) in enumerate(keys):
                        ps, pk = self.bank(lo=2, hi=6)
                        self.mm(pk, ps[:, 0:256], kap, QT[:, h, q0:q0 + 256], True, col is None,
                                rd=ktk + [k_q[j][tq], k_qz])
                        if col is not None:
                            self.mm(pk, ps[:, 0:256], self.identb[:], tab[:, tr, col:col + 256], False, True,
                                    rd=[k_tab[tr], self.k_const])
                        p_ = pti % 3
                        pti += 1
                        self.op("act", lambda e, p_=p_, ps=ps: e.activation(out=PT[:, p_, :], in_=ps[:, 0:256], func=AF.Exp),
                                rd=[pk], wr=[k_pt[p_]])
                        self.mm(pok, po[:, 0:256], VA[:, j, vt, hh * 128:(hh + 1) * 128], PT[:, p_, :], ki == 0,
                                ki == len(keys) - 1, rd=[vtk, k_ones, k_pt[p_]])
                    oh = slice(0, 64) if hh == 0 else slice(64, 128)
                    dh_ = slice(64, 128) if hh == 0 else slice(0, 64)
                    r_ = ri % 2
                    ri += 1
                    self.op("dve", lambda e, r_=r_, po=po: e.reciprocal(out=R[oh, r_, :], in_=po[dh_, 0:256]),
                            rd=[pok], wr=[k_r[r_]])
                    self.op("dve", lambda e, r_=r_, po=po: e.tensor_tensor(out=self.YT[oh, 2 + j, q0:q0 + 256],
                                                                           in0=po[oh, 0:256], in1=R[oh, r_, :], op=ALU.mult),
                            rd=[pok, k_r[r_]], pw=[self.yT_tk[2 + j][tq]])
            finish_halo()

    def phase_swa(self, l, ctx_out):
        with self.scope() as ss:
            QT = self.sb("sQT", [128, 2, 3, NTK], BF16, ss)
            KT = self.sb("sKT", [128, NTK], BF16, ss)
            VA = self.sb("sVA", [128, 19, 192], BF16, ss)
            HB = self.sb("sHB", [128, 256], BF16, ss)
            htmp = self.sb("shtmp", [128, 2, 256], BF16, ss)
            cosT = self.sb("cosT", [128, NT], F32, ss)
            sinT = self.sb("sinT", [128, NT], F32, ss)
            rt = self.sb("srt", [128, 4, 512], F32, ss)
            PT = self.sb("sPT", [128, 3, 384], BF16, ss)
            R = self.sb("sR", [128, 2, 384], F32, ss)
            dtmp = self.sb("sdtmp", [128, 2, 384], F32, ss)
            esT = self.sb("sesT", [128, 2, 384], F32, ss)
            k_q = [[Tk(f"sq{j}_{t}") for t in range(5)] for j in range(3)]
            k_k = [Tk(f"sk{t}") for t in range(5)]
            k_va = [Tk(f"sva{g}") for g in range(6)]
            k_ones, k_hb, k_rope, k_es = Tk("sones"), Tk("sHB"), Tk("rope"), Tk("esT")
            k_htmp = [Tk("sht0"), Tk("sht1")]
            k_rt = [Tk(f"srt{i}") for i in range(4)]
            k_pt = [Tk(f"sPT{i}") for i in range(3)]
            k_r = [Tk("sR0"), Tk("sR1")]
            k_dt = [Tk("sdt0"), Tk("sdt1")]
            self.dma(out=cosT[:], in_=self.d_cosT[:, :], pw=[k_rope])
            self.dma(out=sinT[:], in_=self.d_sinT[:, :], pw=[k_rope])
            self.op("dve", lambda e: e.memset(VA[:, :, :], 1.0), wr=[k_ones])
            k_qz = Tk("sqz")
            self.op("dve", lambda e: e.memset(QT[64:128, 0, :, :], 0.0), pw=[k_qz])
            self.op("dve", lambda e: e.memset(QT[0:64, 1, :, :], 0.0), pw=[k_qz])
            self.op("dve", lambda e: e.memset(esT[:], 0.0), wr=[k_es])
            for g in range(2):
                for i_ in range(3):
                    self.op("dve", lambda e, g=g, i_=i_: e.tensor_scalar(
                        out=esT[:, g, i_ * 128:(i_ + 1) * 128], in0=esT[:, g, i_ * 128:(i_ + 1) * 128],
                        scalar1=self.esink[:, l, 3 * g + i_:3 * g + i_ + 1], scalar2=None, op0=ALU.add),
                        rd=[self.k_const, k_es], wr=[k_es])
            rti = [0]

            def roped(blk, blkp, dsts, tks, tiles):
                w, wk = self.wload(self.d_win[l, blk, :, :], 1024)
                wp, wpk = self.wload(self.d_win[l, blkp, :, :], 1024)
                for t in tiles:
                    t0, n = TT[t]
                    pb, pk = self.bank()
                    for kc in range(8):
                        self.mm(pk, pb[:, 0:n], w[:, kc * 128:(kc + 1) * 128], self.HT[:, kc, t0:t0 + n], kc == 0, kc == 7,
                                rd=[wk, self.hT_tk[kc][t]])
                    if t == 4:
                        for qi_, (dap, rows) in enumerate(dsts(t0, n)):
                            self.op("act", lambda e, dap=dap, rows=rows: e.copy(out=dap, in_=pb[rows, 0:n]), rd=[pk],
                                    wr=[tks[t]] if qi_ == 0 else (), pw=[tks[t]] if qi_ else ())
                        continue
                    pb2, pk2 = self.bank()
                    for kc in range(8):
                        self.mm(pk2, pb2[:, 0:n], wp[:, kc * 128:(kc + 1) * 128], self.HT[:, kc, t0:t0 + n], kc == 0,
                                kc == 7, rd=[wpk, self.hT_tk[kc][t]])
                    a, b_ = rti[0] % 4, (rti[0] + 1) % 4
                    rti[0] += 2
                    self.op("dve", lambda e: e.tensor_tensor(out=rt[:, a, 0:n], in0=pb[:, 0:n], in1=cosT[:, t0:t0 + n],
                                                             op=ALU.mult), rd=[pk, k_rope], wr=[k_rt[a]])
                    self.op("dve", lambda e: e.tensor_tensor(out=rt[:, b_, 0:n], in0=pb2[:, 0:n], in1=sinT[:, t0:t0 + n],
                                                             op=ALU.mult), rd=[pk2, k_rope], wr=[k_rt[b_]])
                    for qi_, (dap, rows) in enumerate(dsts(t0, n)):
                        self.op("pool", lambda e, dap=dap, rows=rows: e.tensor_tensor(
                            out=dap, in0=rt[rows, a, 0:n], in1=rt[rows, b_, 0:n], op=ALU.add),
                            rd=[k_rt[a], k_rt[b_]], wr=[tks[t]] if qi_ == 0 else (), pw=[tks[t]] if qi_ else ())
            tl5 = [0, 1, 2, 3, 4]
            for j in range(3):
                roped(B_SQ + j, B_SQP + j,
                      lambda t0, n, j=j: [(QT[0:64, 0, j, t0:t0 + n], slice(0, 64)),
                                          (QT[64:128, 1, j, t0:t0 + n], slice(64, 128))],
                      k_q[j], tl5 if ctx_out else [0, 1, 2, 3])
            roped(B_SK, B_SKP, lambda t0, n: [(KT[:, t0:t0 + n], slice(0, 128))], k_k, tl5)

            def ev_v(grp, pb, pk):
                i0, ng = grp[0], len(grp)
                src = pb[:, 0:ng * 128].rearrange("p (a b) -> p a b", a=ng)
                eng = "act" if (i0 // 4) % 2 == 0 else "dve"
                for (c0, s0) in ((0, 0), (128, 64)):
                    if eng == "act":
                        self.op("act", lambda e, c0=c0, s0=s0: e.copy(out=VA[:, i0:i0 + ng, c0:c0 + 64],
                                                                      in_=src[:, :, s0:s0 + 64]),
                                rd=[pk, k_ones], pw=[k_va[i0 // 4]])
                    else:
                        self.op("dve", lambda e, c0=c0, s0=s0: e.tensor_copy(out=VA[:, i0:i0 + ng, c0:c0 + 64],
                                                                             in_=src[:, :, s0:s0 + 64]),
                                rd=[pk, k_ones], pw=[k_va[i0 // 4]])
            self.vproj(l, B_SV, list(range(18)), ev_v)
            rcv, cev = self.exchange(f"swa{l}", [(0, KT[:, 1920:2048]), (128, VA[:, 15, 0:64]), (192, VA[:, 15, 128:192])],
                                     256, BF16, rd=[k_k[3], k_va[3]])
            halo_done = [False]

            def finish_halo():
                if halo_done[0]:
                    return
                halo_done[0] = True
                self.recv_blend(rcv, cev, 0, 256, HB[:, :], [k_hb], htmp, k_htmp)
                self.op("pool", lambda e: e.tensor_copy(out=VA[:, 18, 0:64], in_=HB[:, 128:192]), rd=[k_hb], pw=[k_va[5]])
                self.op("pool", lambda e: e.tensor_copy(out=VA[:, 18, 128:192], in_=HB[:, 192:256]), rd=[k_hb], pw=[k_va[5]])

            pti = 0
            ri = 0
            qbs = list(range(16)) + ([16, 17] if ctx_out else [])
            for qb in qbs:
                q0 = qb * 128
                tq = qb // 4
                for g in range(2):
                    gs = slice(g * 64, (g + 1) * 64)
                    keys = []
                    if qb < 16:
                        if qb > 0:
                            keys.append((KT[:, (qb - 1) * 128:qb * 128], [k_k[(qb - 1) // 4]], qb - 1, k_va[(qb - 1) // 4], 0))
                        keys.append((KT[:, qb * 128:(qb + 1) * 128], [k_k[qb // 4]], qb, k_va[qb // 4], None))
                        if qb < 15:
                            keys.append((KT[:, (qb + 1) * 128:(qb + 2) * 128], [k_k[(qb + 1) // 4]], qb + 1,
                                         k_va[(qb + 1) // 4], 1))
                        else:
                            finish_halo()
                            keys.append((HB[:, 0:128], [k_hb], 18, k_va[5], 2))
                    for c_ in range(2):
                        keys.append((KT[:, 2048 + c_ * 128:2048 + (c_ + 1) * 128], [k_k[4]], 16 + c_, k_va[4], None))
                    po, pok = self.bank(lo=0, hi=2)
                    po3 = po[:, 0:384].rearrange("p (a b) -> p a b", a=3)
                    for ki, (kap, ktk, vt, vtk, mk) in enumerate(keys):
                        ps, pk = self.bank(lo=2, hi=6)
                        ps3 = ps[:, 0:384].rearrange("p (a b) -> p a b", a=3)
                        self.mm(pk, ps3, kap, QT[:, g, :, q0:q0 + 128], True, mk is None,
                                rd=ktk + [k_q[j_][tq] for j_ in range(3)] + [k_qz])
                        if mk is not None:
                            self.mm(pk, ps3, self.identb[:],
                                    self.swamask[:, mk * 384:(mk + 1) * 384].rearrange("p (a b) -> p a b", a=3),
                                    False, True, rd=[self.k_const])
                        p_ = pti % 3
                        pti += 1
                        self.op("act", lambda e, p_=p_, ps=ps: e.activation(out=PT[:, p_, :], in_=ps[:, 0:384], func=AF.Exp,
                                                                            scale=0.125), rd=[pk], wr=[k_pt[p_]])
                        self.mm(pok, po[:, 0:384], VA[:, vt, g * 64:g * 64 + 128], PT[:, p_, :], ki == 0,
                                ki == len(keys) - 1, rd=[vtk, k_ones, k_pt[p_]])
                    oh = slice(0, 64) if g == 0 else slice(64, 128)
                    dh_ = slice(64, 128) if g == 0 else slice(0, 64)
                    r_ = ri % 2
                    ri += 1
                    self.op("dve", lambda e, r_=r_, po=po: e.tensor_tensor(out=dtmp[dh_, r_, :], in0=po[dh_, 0:384],
                                                                           in1=esT[dh_, g, :], op=ALU.add),
                            rd=[pok, k_es], wr=[k_dt[r_]])
                    self.op("dve", lambda e, r_=r_: e.reciprocal(out=R[oh, r_, :], in_=dtmp[dh_, r_, :]),
                            rd=[k_dt[r_]], wr=[k_r[r_]])
                    self.op("dve", lambda e, r_=r_, po3=po3: e.tensor_tensor(
                        out=self.YT[oh, 5:8, q0:q0 + 128], in0=po3[oh, :, :],
                        in1=R[oh, r_, :].rearrange("p (a b) -> p a b", a=3), op=ALU.mult),
                        rd=[pok, k_r[r_]], pw=[self.yT_tk[5 + j_][tq] for j_ in range(3)])
            finish_halo()

    def phase_outproj(self, l, xT, tiles):
        for ob in range(8):
            w, wk = self.wload(self.d_wout[l, ob, :, :], 1024)
            for t in tiles:
                t0, n = TT[t]
                wsel = 1 if t == 4 else 0
                pb, pk = self.bank()
                for kc in range(8):
                    self.mm(pk, pb[:, 0:n], w[:, kc * 128:(kc + 1) * 128], self.YT[:, kc, t0:t0 + n], kc == 0, kc == 7,
                            rd=[wk, self.yT_tk[kc][t]])
                self.op("dve", lambda e, pb=pb, t0=t0, n=n, wsel=wsel: e.scalar_tensor_tensor(
                    out=xT[:, ob, t0:t0 + n], in0=pb[:, 0:n], scalar=self.modT[:, 16 + ob, wsel:wsel + 1],
                    in1=xT[:, ob, t0:t0 + n], op0=ALU.mult, op1=ALU.add),
                    rd=[pk, self.k_mod], wr=[self.xT_tk[ob][t]])

    def phase_ffn(self, l, xT, with_ctx):
        if with_ctx:
            groups = [[(0, 512, 0, 0), (512, 512, 1, 0), (2048, 128, 4, 1)],
                      [(1024, 512, 2, 0), (1536, 512, 3, 0), (2176, 128, 4, 1)]]
        else:
            groups = [[(0, 512, 0, 0), (512, 512, 1, 0)], [(1024, 512, 2, 0), (1536, 512, 3, 0)]]
        GW = 1152
        self.fence()
        aT1 = self.YT[:].rearrange("p a b -> p (a b)").rearrange("p (a b) -> p a b", a=16)
        k_aT = [Tk(f"aT{f}") for f in range(NFB)]
        with self.scope() as sf:
            sg = self.sb("fsg", [128, 2, 512], F32, sf)
            aT2 = self.sb("aT2", [128, NFB - 16, GW], BF16, sf)
            k_sg = [Tk("fsg0"), Tk("fsg1")]

            def aT(fb, o, n):
                return aT1[:, fb, o:o + n] if fb < 16 else aT2[:, fb - 16, o:o + n]
            si = 0
            for grp in groups:
                offs = []
                o = 0
                for (t0, n, t, wsel) in grp:
                    offs.append(o)
                    o += n
                for fb in range(NFB):
                    wg, wgk = self.wload(self.d_wgu[l, fb, 0, :, :], 1024)
                    wu, wuk = self.wload(self.d_wgu[l, fb, 1, :, :], 1024)
                    for pi, (t0, n, t, wsel) in enumerate(grp):
                        pg, pgk = self.bank(lo=3, hi=6)
                        for kc in range(8):
                            self.mm(pgk, pg[:, 0:n], wg[:, kc * 128:(kc + 1) * 128], self.HT[:, kc, t0:t0 + n], kc == 0,
                                    kc == 7, rd=[wgk, self.hT_tk[kc][t]])
                        pu, puk = self.bank(lo=3, hi=6)
                        for kc in range(8):
                            self.mm(puk, pu[:, 0:n], wu[:, kc * 128:(kc + 1) * 128], self.HT[:, kc, t0:t0 + n], kc == 0,
                                    kc == 7, rd=[wuk, self.hT_tk[kc][t]])
                        q = si % 2
                        si += 1
                        self.op("act", lambda e, q=q, pg=pg, n=n: e.activation(out=sg[:, q, 0:n], in_=pg[:, 0:n], func=AF.Silu),
                                rd=[pgk], wr=[k_sg[q]])
                        oo = offs[pi]
                        self.op("dve", lambda e, q=q, pu=pu, n=n, oo=oo: e.tensor_tensor(
                            out=aT(fb, oo, n), in0=pu[:, 0:n], in1=sg[:, q, 0:n], op=ALU.mult),
                            rd=[puk, k_sg[q]], wr=[k_aT[fb]] if pi == 0 else (), pw=[k_aT[fb]] if pi else ())
                for ob in range(8):
                    pbs = [self.bank(pi) for pi in range(len(grp))]
                    for half in range(2):
                        wd, wdk = self.wload(self.d_wdn[l, ob, half, :, :], 1408)
                        for pi, (t0, n, t, wsel) in enumerate(grp):
                            pb, pk = pbs[pi]
                            oo = offs[pi]
                            for kc in range(11):
                                kk = half * 11 + kc
                                self.mm(pk, pb[:, 0:n], wd[:, kc * 128:(kc + 1) * 128], aT(kk, oo, n), kk == 0, kk == 21,
                                        rd=[wdk, k_aT[kk]], sig=(kc == 10))
                    for pi, (t0, n, t, wsel) in enumerate(grp):
                        pb, pk = pbs[pi]
                        part = (n != 512)
                        self.op("dve", lambda e, pb=pb, t0=t0, n=n, wsel=wsel: e.scalar_tensor_tensor(
                            out=xT[:, ob, t0:t0 + n], in0=pb[:, 0:n], scalar=self.modT[:, 40 + ob, wsel:wsel + 1],
                            in1=xT[:, ob, t0:t0 + n], op0=ALU.mult, op1=ALU.add),
                            rd=[pk, self.k_mod, self.xT_tk[ob][t]], wr=() if part else [self.xT_tk[ob][t]],
                            pw=[self.xT_tk[ob][t]] if part else ())
        for k in range(8):
            for t in range(5):
                self.yT_tk[k][t].fence = Tk.cur_fence

    def phase_final(self, xT):
        with self.scope() as st:
            sq = self.sb("fsq", [128, 2, 4, 512], BF16, st)
            rs = self.sb("frs", [128, 2, 512], F32, st)
            fin = self.sb("fin", [128, 4, 512], F32, st)
            yo = self.sb("yo", [128, 2, 512], F32, st)
            sq_tk = [Tk("fsq0"), Tk("fsq1")]
            rs_tk = [Tk("frs0"), Tk("frs1")]
            fin_tk = [Tk(f"fin{j}") for j in range(4)]
            yo_tk = [Tk("yo0"), Tk("yo1")]
            hi = 0
            yi = 0
            for t in range(4):
                t0, n = TT[t]
                s = t % 2
                pb, pk = self.bank()
                for half in range(2):
                    hs = hi % 2
                    hi += 1
                    for j in range(4):
                        k = half * 4 + j
                        self.op("act", lambda e, k=k, j=j, hs=hs: e.activation(out=sq[:, hs, j, 0:n], in_=xT[:, k, t0:t0 + n],
                                                                               func=AF.Square),
                                rd=[self.xT_tk[k][t]], wr=[sq_tk[hs]] if j == 0 else (), pw=[sq_tk[hs]] if j else ())
                    for j in range(4):
                        k = half * 4 + j
                        self.mm(pk, pb[:, 0:n], self.onesb[:], sq[:, hs, j, 0:n], k == 0, k == 7,
                                rd=[sq_tk[hs], self.k_const], sig=(j == 3))
                self.op("act", lambda e, pb=pb: e.activation(out=rs[:, s, 0:n], in_=pb[:, 0:n], func=AF.Sqrt,
                                                             bias=self.epsT[:, 0:1], scale=1.0 / D),
                        rd=[pk, self.k_const], wr=[rs_tk[s]])
                self.op("dve", lambda e: e.reciprocal(out=rs[:, s, 0:n], in_=rs[:, s, 0:n]), rd=[rs_tk[s]], wr=[rs_tk[s]])
                for half in range(2):
                    for j in range(4):
                        k = half * 4 + j
                        self.op("dve", lambda e, k=k, j=j: e.scalar_tensor_tensor(
                            out=fin[:, j, 0:n], in0=xT[:, k, t0:t0 + n], scalar=self.fnorm[:, k:k + 1],
                            in1=rs[:, s, 0:n], op0=ALU.mult, op1=ALU.mult),
                            rd=[self.xT_tk[k][t], rs_tk[s], self.k_const], wr=[fin_tk[j]])
                    for sub in range(4):
                        i = t * 4 + sub
                        pt, ptk = self.bank()
                        for j in range(4):
                            self.op("pe", lambda e, j=j, pt=pt, sub=sub: e.transpose(
                                out=pt[:, j * 128:(j + 1) * 128], in_=fin[:, j, sub * 128:(sub + 1) * 128],
                                identity=self.identf[:]),
                                rd=[fin_tk[j], self.k_const], wr=[ptk] if j == 0 else (), pw=[ptk] if j else (),
                                sig=(j == 3))
                        y_ = yi % 2
                        yi += 1
                        if yi % 2:
                            self.op("act", lambda e, pt=pt, y_=y_: e.copy(out=yo[:, y_, :], in_=pt[:, 0:512]),
                                    rd=[ptk], wr=[yo_tk[y_]])
                        else:
                            self.op("dve", lambda e, pt=pt, y_=y_: e.tensor_copy(out=yo[:, y_, :], in_=pt[:, 0:512]),
                                    rd=[ptk], wr=[yo_tk[y_]])
                        self.out_evs.append(self.dma(out=self.d_y[i * 128:(i + 1) * 128, half * 512:(half + 1) * 512],
                                                     in_=yo[:, y_, :], rd=[yo_tk[y_]]))

    def build(self):
        self.setup()
        nc = self.nc
        upto = self.upto
        allh = [self.hT_tk[k][t] for k in range(8) for t in range(5)]
        ally = [self.yT_tk[k][t] for k in range(8) for t in range(5)]
        k_xs = [Tk(f"xsave{k}") for k in range(8)]
        st = self.scope()
        st.__enter__()
        xT = self.sb("xT", [128, 8, NTK], F32, st)
        self.phase_load(xT)
        try:
            for l in range(DEPTH):
                ctx_out = (l == 0)
                tiles = [0, 1, 2, 3, 4] if ctx_out else [0, 1, 2, 3]
                self.phase_mod(l)
                self.phase_norm(xT, self.A1, 0, [0, 1, 2, 3, 4])
                if upto == "norm":
                    self.dump("hT", self.HT[:], [128, 8, NTK], BF16, allh)
                    raise _Stop()
                for k in range(8):
                    self.dma(out=self.d_xsave[:, k * NTK:(k + 1) * NTK], in_=xT[:, k, :],
                             rd=[self.xT_tk[k][t] for t in range(5)], wr=[k_xs[k]], q="act")
                st.__exit__(None, None, None)
                if upto == "save":
                    raise _Stop()
                if upto in ("na", "swa", "n_proj", "n_exch", "n_h0"):
                    if upto != "swa":
                        self.phase_na(0, True)
                        self.dump("yT", self.YT[:, 2:5], [128, 3, NTK], BF16, ally)
                    else:
                        self.phase_swa(0, True)
                        self.dump("yT", self.YT[:, 5:8], [128, 3, NTK], BF16, ally)
                    raise _Stop()
                self.phase_gla(l, ctx_out)
                if upto == "gla":
                    self.dump("yT", self.YT[:, 0:2], [128, 2, NTK], BF16, ally)
                    raise _Stop()
                self.phase_na(l, ctx_out)
                self.phase_swa(l, ctx_out)
                if upto == "mix":
                    self.dump("yT", self.YT[:], [128, 8, NTK], BF16, ally)
                    raise _Stop()
                st = self.scope()
                st.__enter__()
                xT = self.sb("xT", [128, 8, NTK], F32, st)
                for k in range(8):
                    for t in range(5):
                        tk = self.xT_tk[k][t]
                        tk.w, tk.r, tk.base, tk.fence = {}, {}, None, Tk.cur_fence
                    self.dma(out=xT[:, k, :], in_=self.d_xsave[:, k * NTK:(k + 1) * NTK], rd=[k_xs[k]],
                             wr=[self.xT_tk[k][t] for t in range(5)], q="act")
                self.phase_outproj(l, xT, tiles)
                self.phase_norm(xT, self.A2, 24, tiles)
                self.phase_ffn(l, xT, ctx_out)
                if upto == f"layer{l}":
                    self.dump("xT", xT[:], [128, 8, NTK], F32, [self.xT_tk[k][t] for k in range(8) for t in range(5)])
                    raise _Stop()
            self.phase_final(xT)
            st.__exit__(None, None, None)
        except _Stop:
            for nm in ("pe", "act", "dve", "pool"):
                E = self.engs[nm]
                if E.count:
                    self.engs["sp"].wait(Ev(E.sem, E.count, E))
        self.finish()
        return nc


def _run(prog, maps):
    names = set(prog.din.keys())
    in_maps = [{k: v for k, v in m.items() if k in names} for m in maps]
    return run_bass_kernel_spmd(prog.nc, in_maps, core_ids=list(range(8)))


_CACHE = {}


def kernel(**inputs):
    maps = prep_inputs(inputs)
    if "prog" not in _CACHE:
        p = Prog()
        p.build()
        _CACHE["prog"] = p
    p = _CACHE["prog"]
    res = _run(p, maps)
    out = np.empty((4, SEQ, D), np.float32)
    for c in range(8):
        b, par = c // 2, c % 2
        y = np.asarray(res.results[c]["y"])
        if par == 0:
            out[b, 0:NT] = y
        else:
            out[b, NT:SEQ] = y[::-1]
    return out
```
